# Optimizing a Trainium2 kernel written in Bass

```python
import math
import jax, jax.numpy as jnp
from jax import lax
import numpy as np

D_MODEL = 1024
BATCH = 8
SEQ = 2048
DEPTH = 1

CHUNK = 64
POOL_WINDOWS = (2, 4, 8, 16)
POOL_GROUPS = len(POOL_WINDOWS)
POOL_WIDTH = D_MODEL // 2
POOL_GROUP_DIM = POOL_WIDTH // POOL_GROUPS
RET_HEADS = 4
RET_WIDTH = D_MODEL // 2
RET_HEAD_DIM = RET_WIDTH // RET_HEADS
N_BRANCHES = 2
IN_WIDTH = POOL_WIDTH + 4 * RET_WIDTH + N_BRANCHES * D_MODEL
D_FF = 4 * D_MODEL
N_MOD = 6
ROPE_BASE = 10000.0
EPS = 1e-6

kernel_name = "chunk_causal_pool_retention_hybrid_block"


def _rmsnorm(x, g):
    xf = x.astype(jnp.float32)
    y = xf * lax.rsqrt(jnp.mean(xf * xf, axis=-1, keepdims=True) + EPS)
    return (y * g.astype(jnp.float32)).astype(x.dtype)


def _rope(x, positions):
    half = x.shape[-1] // 2
    inv_freq = ROPE_BASE ** (-jnp.arange(half, dtype=jnp.float32) / half)
    ang = positions.astype(jnp.float32)[..., None] * inv_freq
    cos = jnp.cos(ang)[:, :, None, :].astype(x.dtype)
    sin = jnp.sin(ang)[:, :, None, :].astype(x.dtype)
    x1, x2 = x[..., :half], x[..., half:]
    return jnp.concatenate([x1 * cos - x2 * sin, x2 * cos + x1 * sin], axis=-1)


def _pool_branch(u, w_group, scale):
    B, S, _ = u.shape
    uf = u.astype(jnp.float32)
    cs = jnp.cumsum(uf, axis=1)
    t = jnp.arange(1, S + 1, dtype=jnp.float32)[None, :, None]
    outs = []
    for gi, w in enumerate(POOL_WINDOWS):
        sl = slice(gi * POOL_GROUP_DIM, (gi + 1) * POOL_GROUP_DIM)
        cg = cs[..., sl]
        lag = jnp.pad(cg[:, :S - w], ((0, 0), (w, 0), (0, 0)))
        mean = (cg - lag) / jnp.minimum(t, float(w))
        outs.append(mean - uf[..., sl])
    pooled = jnp.stack(outs, axis=2).astype(u.dtype)
    mixed = jnp.einsum('bsgc,gcd->bsgd', pooled, w_group).reshape(B, S, POOL_WIDTH)
    return mixed * scale


def _retention_branch(q, k, v, g, positions):
    B, S, _ = q.shape
    N = S // CHUNK
    H, d = RET_HEADS, RET_HEAD_DIM
    q = _rope(q.reshape(B, S, H, d), positions) * (d ** -0.5)
    k = _rope(k.reshape(B, S, H, d), positions)
    v = v.reshape(B, S, H, d)
    log_gamma = jnp.log(1.0 - 2.0 ** (-5.0 - jnp.arange(H, dtype=jnp.float32)))
    idx = jnp.arange(CHUNK, dtype=jnp.float32)
    dist = jnp.abs(idx[:, None] - idx[None, :])
    d_intra = jnp.exp(log_gamma[:, None, None] * dist).astype(q.dtype)
    k_decay = jnp.exp(log_gamma[None, :] * (CHUNK - 1 - idx)[:, None]).astype(q.dtype)
    q_decay = jnp.exp(log_gamma[None, :] * (idx + 1)[:, None]).astype(q.dtype)
    chunk_decay = jnp.exp(log_gamma * CHUNK).astype(q.dtype)

    qc = q.reshape(B, N, CHUNK, H, d)
    kc = k.reshape(B, N, CHUNK, H, d)
    vc = v.reshape(B, N, CHUNK, H, d)

    scores = jnp.einsum('bnihd,bnjhd->bnhij', qc, kc) * d_intra
    o_intra = jnp.einsum('bnhij,bnjhe->bnihe', scores, vc)

    kv = jnp.einsum('bnjhd,bnjhe->nbhde', kc * k_decay[:, :, None], vc)

    def step(state, kv_n):
        return state * chunk_decay[None, :, None, None] + kv_n, state

    _, states = lax.scan(step, jnp.zeros_like(kv[0]), kv)
    o_cross = jnp.einsum('bnihd,nbhde->bnihe', qc * q_decay[:, :, None], states)

    o = (o_intra + o_cross).reshape(B, S, H, d).astype(jnp.float32)
    mu = jnp.mean(o, axis=-1, keepdims=True)
    var = jnp.mean(jnp.square(o - mu), axis=-1, keepdims=True)
    o_norm = ((o - mu) * lax.rsqrt(var + EPS)).reshape(B, S, RET_WIDTH).astype(g.dtype)
    return jax.nn.silu(g) * o_norm


def _mixer(h, positions, w_in, b_branch_gate, pool_w, pool_scale,
           w_branch_pool, w_branch_ret, w_out):
    B, S, _ = h.shape
    proj = h @ w_in
    cuts = [POOL_WIDTH + i * RET_WIDTH for i in range(5)]
    u_pool, q, k, v, g_ret, gate_logits = jnp.split(proj, cuts, axis=-1)
    y_pool = _pool_branch(u_pool, pool_w, pool_scale) @ w_branch_pool
    y_ret = _retention_branch(q, k, v, g_ret, positions) @ w_branch_ret
    gates = jax.nn.sigmoid(gate_logits + b_branch_gate).reshape(B, S, N_BRANCHES, D_MODEL)
    merged = gates[:, :, 0] * y_pool + gates[:, :, 1] * y_ret
    return merged @ w_out


def setup_inputs(seed: int = 0) -> dict:
    key = jax.random.key(seed)
    ks = jax.random.split(key, 20)
    f32 = jnp.float32

    def nrm(k, shape, scale):
        return jax.random.normal(k, shape, f32) * scale

    def gain(k, shape):
        return 1.0 + 0.05 * jax.random.normal(k, shape, f32)

    x = jax.random.normal(ks[0], (BATCH, SEQ, D_MODEL), f32)
    c = jax.random.normal(ks[1], (BATCH, D_MODEL), f32)
    offsets = jax.random.randint(ks[2], (BATCH, 1), 0, 4096, dtype=jnp.int32)
    positions = offsets + jnp.arange(SEQ, dtype=jnp.int32)[None, :]
    return {
        "x": x,
        "c": c,
        "positions": positions,
        "ada_w": nrm(ks[3], (DEPTH, D_MODEL, N_MOD * D_MODEL), 0.5 * D_MODEL ** -0.5),
        "ada_b": nrm(ks[4], (DEPTH, N_MOD * D_MODEL), 0.02),
        "mix_pre_g": gain(ks[5], (DEPTH, D_MODEL)),
        "mix_post_g": gain(ks[6], (DEPTH, D_MODEL)),
        "ffn_pre_g": gain(ks[7], (DEPTH, D_MODEL)),
        "ffn_post_g": gain(ks[8], (DEPTH, D_MODEL)),
        "w_in": nrm(ks[9], (DEPTH, D_MODEL, IN_WIDTH), D_MODEL ** -0.5),
        "b_branch_gate": nrm(ks[10], (DEPTH, N_BRANCHES * D_MODEL), 0.02),
        "pool_w": nrm(ks[11], (DEPTH, POOL_GROUPS, POOL_GROUP_DIM, POOL_GROUP_DIM), POOL_GROUP_DIM ** -0.5),
        "pool_scale": 1.0 + 0.1 * jax.random.normal(ks[12], (DEPTH, POOL_WIDTH), f32),
        "w_branch_pool": nrm(ks[13], (DEPTH, POOL_WIDTH, D_MODEL), POOL_WIDTH ** -0.5),
        "w_branch_ret": nrm(ks[14], (DEPTH, RET_WIDTH, D_MODEL), RET_WIDTH ** -0.5),
        "w_out": nrm(ks[15], (DEPTH, D_MODEL, D_MODEL), D_MODEL ** -0.5),
        "w_ff1": nrm(ks[16], (DEPTH, D_MODEL, D_FF), D_MODEL ** -0.5),
        "w_ff2": nrm(ks[17], (DEPTH, D_FF, D_MODEL), D_FF ** -0.5),
    }


def reference(x, c, positions, ada_w, ada_b, mix_pre_g, mix_post_g, ffn_pre_g, ffn_post_g,
              w_in, b_branch_gate, pool_w, pool_scale, w_branch_pool, w_branch_ret, w_out,
              w_ff1, w_ff2):
    for l in range(DEPTH):
        mod = jax.nn.silu(c) @ ada_w[l] + ada_b[l]
        sh_m, sc_m, gt_m, sh_f, sc_f, gt_f = [m[:, None, :] for m in jnp.split(mod, N_MOD, axis=-1)]

        h = _rmsnorm(x, mix_pre_g[l]) * (1.0 + sc_m) + sh_m
        y = _mixer(h, positions, w_in[l], b_branch_gate[l], pool_w[l], pool_scale[l],
                   w_branch_pool[l], w_branch_ret[l], w_out[l])
        x = x + gt_m * _rmsnorm(y, mix_post_g[l])

        h = _rmsnorm(x, ffn_pre_g[l]) * (1.0 + sc_f) + sh_f
        y = jnp.square(jax.nn.relu(h @ w_ff1[l])) @ w_ff2[l]
        x = x + gt_f * _rmsnorm(y, ffn_post_g[l])
    return x
```

```python
import contextlib
import numpy as np
import ml_dtypes
import concourse.bass as bass
import concourse.mybir as mybir
from concourse.bass_utils import run_bass_kernel_spmd

F32 = mybir.dt.float32
BF16 = mybir.dt.bfloat16
I32 = mybir.dt.int32
U8 = mybir.dt.uint8
AF = mybir.ActivationFunctionType
ALU = mybir.AluOpType
PI = float(np.pi)
EPS = 1e-6
S_TOK = 2048
D = 1024
NT = 16
NB = 4
KB = 1024


class Buf:
    __slots__ = ("name", "last_write", "reads")

    def __init__(self, name):
        self.name = name
        self.last_write = None
        self.reads = []


class DmaKey:
    def __init__(self, name, wait_total=False):
        self.name = name
        self.count = 0
        self.wait_total = wait_total
        self.sem = None


class Op:
    __slots__ = ("eng", "fn", "deps", "signal", "seq", "dma_key", "dma_count", "name")


class Sched:
    ENGS = ("pe", "act", "dve", "pool", "sp")

    def __init__(self, same_engine_sync=()):
        self.ops = []
        self.same_engine_sync = set(same_engine_sync)
        self.keys = []

    def key(self, name, wait_total=False):
        k = DmaKey(name, wait_total)
        self.keys.append(k)
        return k

    def op(self, eng, fn, reads=(), writes=(), dma_key=None, name=""):
        o = Op()
        o.eng = eng; o.fn = fn; o.signal = False; o.seq = None
        o.dma_key = dma_key; o.name = name
        if dma_key is not None:
            dma_key.count += 16
            o.dma_count = dma_key.count
        else:
            o.dma_count = None
        deps = []
        for b in reads:
            if b.last_write is not None:
                deps.append(b.last_write)
        for b in writes:
            if b.last_write is not None:
                deps.append(b.last_write)
            deps.extend(b.reads)
        fd = []
        seen = set()
        for d in deps:
            if id(d) in seen or d is o:
                continue
            seen.add(id(d))
            if d.dma_key is None and d.eng == eng and eng not in self.same_engine_sync:
                continue
            if d.dma_key is not None and d.dma_key is dma_key and dma_key.wait_total:
                continue
            fd.append(d)
        o.deps = fd
        for d in fd:
            if d.dma_key is None:
                d.signal = True
        for b in reads:
            b.reads.append(o)
        for b in writes:
            b.last_write = o
            b.reads = []
        self.ops.append(o)
        return o

    def emit(self, block, sems, final_wait_keys=()):
        cnt = {e: 0 for e in self.ENGS}
        for o in self.ops:
            if o.dma_key is None and o.signal:
                cnt[o.eng] += 1
                o.seq = cnt[o.eng]
        by_eng = {e: [o for o in self.ops if o.eng == e] for e in self.ENGS}

        def run(eng_name, eng):
            waited = {}
            for o in by_eng[eng_name]:
                for d in o.deps:
                    if d.dma_key is not None:
                        k = d.dma_key
                        val = k.count if k.wait_total else d.dma_count
                        kk = ("k", id(k))
                        if waited.get(kk, 0) >= val:
                            continue
                        waited[kk] = val
                        eng.wait_ge(k.sem, val)
                    else:
                        kk = ("e", d.eng)
                        if waited.get(kk, 0) >= d.seq:
                            continue
                        waited[kk] = d.seq
                        eng.wait_ge(sems[d.eng], d.seq)
                ins = o.fn(eng)
                if o.dma_key is not None:
                    ins.then_inc(o.dma_key.sem, 16)
                elif o.signal:
                    ins.then_inc(sems[eng_name], 1)
            if eng_name == "sp":
                for k in final_wait_keys:
                    eng.wait_ge(k.sem, k.count)

        @block.tensor
        def _(e):
            run("pe", e)

        @block.scalar
        def _(e):
            run("act", e)

        @block.vector
        def _(e):
            run("dve", e)

        @block.gpsimd
        def _(e):
            run("pool", e)

        @block.sync
        def _(e):
            run("sp", e)


POOL_WINDOWS = (2, 4, 8, 16)
CF_INVF, CF_DQ, CF_DV, CF_CD, CF_IDENT, CF_MASK = 0, 64, 576, 580, 1092, 1220
CF_N = 1732
CB_IDENT, CB_POOLA = 0, 128
CB_N = 128 + 4 * 3 * 128
CC_C, CC_ADAB, CC_GPM, CC_GQM, CC_GPF, CC_GQF, CC_BG, CC_PS = 0, 8, 56, 64, 72, 80, 88, 104
CC_N = 108


def _consts():
    cf = np.zeros((128, CF_N), np.float32)
    invf = (10000.0 ** (-(np.arange(64, dtype=np.float64) / 64.0))).astype(np.float32)
    cf[:, CF_INVF:CF_INVF + 64] = invf[None, :]
    gam = 1.0 - 2.0 ** (-5.0 - np.arange(4, dtype=np.float64))
    il = np.arange(128, dtype=np.float64)
    for h in range(4):
        cf[:, CF_DQ + h * 128:CF_DQ + (h + 1) * 128] = (gam[h] ** (il + 1) / np.sqrt(128.0))[None, :]
        cf[:, CF_DV + h] = gam[h] ** (-(il + 1))
        cf[:, CF_CD + h * 128:CF_CD + (h + 1) * 128] = gam[h] ** 128
        j = il[:, None]; i = il[None, :]
        same = (np.floor(j / 64) == np.floor(i / 64))
        m = np.where(j <= i, 1.0, np.where(same, gam[h] ** (2 * (j - i)), 0.0))
        cf[:, CF_MASK + h * 128:CF_MASK + (h + 1) * 128] = m
    cf[:, CF_IDENT:CF_IDENT + 128] = np.eye(128)
    cb = np.zeros((128, CB_N), np.float32)
    cb[:, CB_IDENT:CB_IDENT + 128] = np.eye(128)
    s = np.arange(128)[:, None]; t = np.arange(128)[None, :]
    for gi, w in enumerate(POOL_WINDOWS):
        cnt0 = np.minimum(t + 1, w).astype(np.float64)
        d0 = np.where((s <= t) & (s > t - w), 1.0 / cnt0, 0.0) - (s == t)
        dg = np.where((s <= t) & (s > t - w), 1.0 / w, 0.0) - (s == t)
        off = np.where(s > 128 + t - w, 1.0 / w, 0.0)
        base = CB_POOLA + gi * 3 * 128
        cb[:, base:base + 128] = d0
        cb[:, base + 128:base + 256] = dg
        cb[:, base + 256:base + 384] = off
    cb_bf = cb.astype(ml_dtypes.bfloat16)
    cb2 = np.zeros((128, 512), np.float32)
    for gi in range(4):
        base = CB_POOLA + gi * 3 * 128
        hi = cb_bf[:, base:base + 128].astype(np.float32)
        cb2[:, gi * 128:(gi + 1) * 128] = cb[:, base:base + 128] - hi
    return cf, cb_bf, cb2.astype(ml_dtypes.bfloat16)


def build(debug=False):
    nc = bass.Bass("TRN2", target_bir_lowering=False, dynamic_dma_scratch_size=8192)

    def din(name, shape, dt=F32):
        return nc.dram_tensor(name, shape, dt, kind="ExternalInput").ap()

    x_d = din("x", [S_TOK, D])
    pos_d = din("pos", [128, NT], I32)
    cc_d = din("cc", [128, CC_N])
    cf_d = din("cf", [128, CF_N])
    cb_d = din("cb", [128, CB_N], BF16)
    cb2_d = din("cb2", [128, 512], BF16)
    adaw_d = din("ada_w", [D, 6 * D])
    win_d = din("w_in", [D, 4608])
    poolw_d = din("pool_w", [128, 4, 128])
    wbp_d = din("w_bp", [512, D])
    wbr_d = din("w_br", [512, D])
    wout_d = din("w_out", [D, D])
    wff1_d = din("w_ff1", [D, 4 * D])
    wff2_d = din("w_ff2", [4 * D, D])
    out_d = nc.dram_tensor("out", [S_TOK, D], F32, kind="ExternalOutput").ap()
    x1_d = nc.dram_tensor("x1d", [S_TOK, D], F32, kind="ExternalOutput" if debug else "Internal").ap()
    scr_d = nc.dram_tensor("scr", [2, D], F32, kind="Internal").ap()
    dbg = {}

    S = Sched(same_engine_sync=("act", "dve", "pool"))
    es = contextlib.ExitStack()
    with es:
        ARENA = 208 * KB
        arena = es.enter_context(nc.sbuf_tensor("arena", [128, ARENA], U8))
        a0 = nc.sbuf_base - ARENA

        def at(name, off, shape, dt):
            return nc.alloc_sbuf_tensor_at(name, shape, dt, offset=a0 + off)

        o = 0
        def take(name, shape, dt, nbytes):
            nonlocal o
            t_ = at(name, o, shape, dt)
            o += (nbytes + 31) // 32 * 32
            return t_
        cc = take("cc", [128, CC_N], F32, CC_N * 4)
        cb = take("cb", [128, CB_N], BF16, CB_N * 2)
        pos_i = take("pos_i", [128, NT], I32, 64)
        pos_f = take("pos_f", [128, NT], F32, 64)
        am = take("am", [128, 8], F32, 32); sm = take("sm", [128, 8], F32, 32)
        af = take("af", [128, 8], F32, 32); sf = take("sf", [128, 8], F32, 32)
        cmc = take("cmc", [128, 8], F32, 32); cfc = take("cfc", [128, 8], F32, 32)
        scb = take("scb", [128, 8, 2], BF16, 32)
        epsb = take("epsb", [128, 1], F32, 32)
        onec = take("onec", [128, 1], F32, 32)
        zeroc = take("zeroc", [128, 1], F32, 32)
        stat = take("stat", [128, 64], F32, 256)
        gst = take("gst", [128, 2, 4, 6], F32, 192)
        gmv = take("gmv", [128, 2, 4, 2], F32, 64)
        grs = take("grs", [128, 2, 4], F32, 32)
        gsq = take("gsq", [128, 2, 4], F32, 32)
        gnb = take("gnb", [128, 2, 4], F32, 32)
        crow = take("crow", [8, 256], F32, 1024)
        modsb = take("modsb", [128, 48], F32, 192)
        cb2 = at("cb2", o - 1024 - 192, [128, 512], BF16)
        assert o <= 6 * KB, o
        coef_f = at("coef_f", 6 * KB, [128, D], F32)
        coef_m = at("coef_m", 10 * KB, [128, D], F32)
        relu_sc = [at(f"relu{i}", 10 * KB + i * 2 * KB, [128, 512], F32) for i in range(2)]
        xs = [at(f"xs{i}", 14 * KB + i * 4 * KB, [128, D], F32) for i in range(2)]
        xn = [at(f"xn{i}", 22 * KB + i * 2 * KB, [128, D], BF16) for i in range(2)]
        tmp = at("tmp", 26 * KB, [128, D], F32)
        TB = 30 * KB
        cos_t = at("cos_t", TB, [128, NT, 64], F32)
        sin_t = at("sin_t", TB + 4 * KB, [128, NT, 64], F32)
        qrot = [at(f"qrot{i}", TB + 8 * KB + i * KB, [128, 512], BF16) for i in range(4)]
        sTm = [at(f"sTm{i}", TB + 12 * KB + i * KB, [128, 512], BF16) for i in range(2)]
        Sbf = [at(f"Sbf{i}", TB + 14 * KB + i * KB, [128, 512], BF16) for i in range(2)]
        Rst = at("Rst", 14 * KB, [128, 512], F32)
        onb = [at(f"onb{i}", 16 * KB + i * KB, [128, 512], BF16) for i in range(2)]
        wo_slab = [at(f"wo{i}", TB + i * 8 * KB, [128, 8, 512], BF16) for i in range(2)]
        h2T = [at(f"h2T{i}", TB + i * 8 * KB, [128, 8, 512], BF16) for i in range(2)]
        MB = 46 * KB
        xn_all = at("xn_all", MB, [128, NT, D], BF16)
        pooledT = at("pooledT", MB + 16 * KB, [128, 4, S_TOK], BF16)
        HT_ = NT // 2
        qT = at("qT", MB, [128, 4, HT_ * 128], BF16)
        kT = at("kT", MB + 8 * KB, [128, 4, HT_ * 128], BF16)
        k_tok = at("k_tok", MB + 16 * KB, [128, HT_, 512], BF16)
        v_tok = at("v_tok", MB + 24 * KB, [128, HT_, 512], BF16)
        mergedT = at("mergedT", MB, [128, NB, 8, 512], BF16)
        W1B = 78 * KB
        hT = at("hT", W1B, [128, NB, 8, 512], BF16)
        sgT = at("sgT", W1B + 32 * KB, [128, 4, S_TOK], BF16)
        gate_sg = [at(f"gate_sg{i}", W1B + 32 * KB + i * 8 * KB, [128, 8, 512], BF16) for i in range(2)]
        cf = at("cf", W1B + 48 * KB, [128, CF_N], F32)
        assert CF_N * 4 <= 7 * KB
        poolw = at("poolw", W1B + 55 * KB, [128, 4, 128], BF16)
        esc = [at(f"esc{i}", W1B + 48 * KB + i * 2 * KB, [128, 512], F32) for i in range(8)]
        w1 = [at(f"w1_{i}", W1B + i * 8 * KB, [128, 8, 512], BF16) for i in range(8)]
        W2B = 142 * KB
        mixedT = at("mixedT", W2B, [128, NB, 4, 512], BF16)
        u_tok = at("u_tok", W2B, [128, NT, 512], BF16)
        retoutT = at("retoutT", W2B + 16 * KB, [128, NB, 4, 512], BF16)
        ring = [at(f"ring{i}", W2B + 32 * KB + i * 8 * KB, [128, 8, 512], BF16) for i in range(4)]
        wbp_s = at("wbp_s", W2B + 48 * KB, [128, 8, 512], BF16)
        wbr_s = at("wbr_s", W2B + 56 * KB, [128, 8, 512], BF16)
        w2 = [at(f"w2_{i}", (W2B + i * 8 * KB) if i < 4 else (MB + (i - 4) * 8 * KB), [128, 4, D], BF16) for i in range(8)]
        hidden = at("hidden", W2B + 32 * KB, [128, 32, 512], BF16)
        junk = at("junk", 206 * KB, [128, D], BF16)

        ps = es.enter_context(nc.psum_tensor("ps", [128, 8, 512], F32))
        sems = {e: es.enter_context(nc.semaphore("s_" + e)) for e in Sched.ENGS}

        def mkkey(name, wait_total=False):
            k = S.key(name, wait_total)
            k.sem = es.enter_context(nc.semaphore("k_" + name))
            return k

        k_const = mkkey("const", True)
        k_xs = [mkkey(f"xs{i}") for i in range(2)]
        k_ring = [mkkey(f"ring{i}") for i in range(4)]
        k_wo = [mkkey(f"wo{i}") for i in range(2)]
        k_w1 = [mkkey(f"w1_{i}") for i in range(8)]
        k_w2 = [mkkey(f"w2_{i}") for i in range(8)]
        k_misc = mkkey("misc")
        k_scr = [mkkey("scr0"), mkkey("scr1")]
        k_wb = [mkkey("wbp"), mkkey("wbr")]
        dbg_keys = []

        Bc = Buf("consts"); Bmod = Buf("mod"); Bcoef = [Buf("coef_m"), Buf("coef_f")]; Bscr = [Buf("scr0"), Buf("scr1")]
        Bams = Buf("am_sm"); Bafs = Buf("af_sf"); Bcrow = Buf("crow"); Bcmc = Buf("cmc")
        Bxs = [Buf(f"xs{i}") for i in range(2)]; Bxn = [Buf(f"xn{i}") for i in range(2)]; Btmp = Buf("tmp")
        Bstat = [Buf(f"stat{i}") for i in range(16)]
        Bmodsb = Buf("modsb")
        Bjunk = Buf("junk")
        Bcf = Buf("cf"); Bwb = [Buf("wbp"), Buf("wbr")]
        Bps = [Buf(f"ps{i}") for i in range(8)]
        Bxna = [Buf(f"xna{t}") for t in range(NT)]
        BhT = [Buf(f"hT{t}") for t in range(NT)]
        Bring = [Buf(f"ring{i}") for i in range(4)]
        Bwo = [Buf(f"wo{i}") for i in range(2)]
        Bw1 = [Buf(f"w1_{i}") for i in range(8)]
        Bw2 = [Buf(f"w2_{i}") for i in range(8)]
        Btab = Buf("tables"); Brsc = Buf("rsc"); Bqrot = [Buf(f"qrot{i}") for i in range(4)]
        Bu = [Buf(f"u{t}") for t in range(NT)]; Bk = [Buf(f"k{t}") for t in range(NT)]; Bv = [Buf(f"v{t}") for t in range(NT)]
        BqT = [Buf(f"qT{t}") for t in range(NT)]; BkT = [Buf(f"kT{t}") for t in range(NT)]
        Bsg = [[Buf(f"sg{h}_{b}") for b in range(NB)] for h in range(4)]
        Bpl = [[Buf(f"pl{g}_{b}") for b in range(NB)] for g in range(4)]
        Bmx = [[Buf(f"mx{g}_{b}") for b in range(NB)] for g in range(4)]
        Bro = [Buf(f"ro{t}") for t in range(NT)]
        BsTm = [Buf("sTm0"), Buf("sTm1")]; BR = Buf("R"); BSbf = [Buf("Sbf0"), Buf("Sbf1")]; Bon = [Buf("on0"), Buf("on1")]
        Bgn = [Buf("gn0"), Buf("gn1")]
        Bgst = [Buf("gst0"), Buf("gst1")]; Bgmv = [Buf("gmv0"), Buf("gmv1")]; Bgsq = [Buf("gsq0"), Buf("gsq1")]
        Bgrs = [Buf("grs0"), Buf("grs1")]; Bgnb = [Buf("gnb0"), Buf("gnb1")]
        Besc = [Buf(f"esc{i}") for i in range(8)]
        Bmg = [[Buf(f"mg{c}_{b}") for b in range(NB)] for c in range(8)]
        Bx1d = [Buf(f"x1d{t}") for t in range(NT)]
        Bh2 = [Buf("h2T0"), Buf("h2T1")]
        Bhid = [Buf(f"hid{f}") for f in range(32)]
        Brelu = [Buf("relu0"), Buf("relu1")]
        Bpoolw = Buf("poolw")

        def dma(eng, out, in_, reads, writes, key):
            S.op(eng, lambda e: e.dma_start(out=out, in_=in_), reads, writes, dma_key=key)

        def act(out, in_, func, reads, writes, **kw):
            S.op("act", lambda e: e.activation(out=out, in_=in_, func=func, **kw), reads, writes)

        def tt(out, in0, in1, op, reads, writes):
            S.op("dve", lambda e: e.tensor_tensor(out=out, in0=in0, in1=in1, op=op), reads, writes)

        def ts(out, in0, s1, s2, op0, op1, reads, writes):
            if op1 is None:
                S.op("dve", lambda e: e.tensor_scalar(out=out, in0=in0, scalar1=s1, scalar2=None, op0=op0), reads, writes)
            else:
                S.op("dve", lambda e: e.tensor_scalar(out=out, in0=in0, scalar1=s1, scalar2=s2, op0=op0, op1=op1), reads, writes)

        def stt(out, in0, scalar, in1, op0, op1, reads, writes):
            S.op("dve", lambda e: e.scalar_tensor_tensor(out=out, in0=in0, scalar=scalar, in1=in1, op0=op0, op1=op1), reads, writes)

        def cp(out, in_, reads, writes):
            S.op("dve", lambda e: e.tensor_copy(out=out, in_=in_), reads, writes)

        def mms(lst, reads, writes):
            def fn(e):
                ins = None
                for (o_, l_, r_, st_, sp_) in lst:
                    ins = e.matmul(o_, lhsT=l_, rhs=r_, start=st_, stop=sp_)
                return ins
            S.op("pe", fn, reads, writes)

        def trs(lst, reads, writes, ident):
            def fn(e):
                ins = None
                for (o_, i_) in lst:
                    ins = e.transpose(out=o_, in_=i_, identity=ident)
                return ins
            S.op("pe", fn, reads, writes)

        def evac_scaled(dst_fn, pv, a_t, s_t, rd, wr, n_act=3):
            for c in range(8):
                src = pv[:, c * 128:(c + 1) * 128]
                if c < n_act:
                    act(dst_fn(c), src, AF.Identity, rd, wr, scale=a_t[:, c:c + 1], bias=s_t[:, c:c + 1])
                else:
                    ts(dst_fn(c), src, a_t[:, c:c + 1], s_t[:, c:c + 1], ALU.mult, ALU.add, rd, wr)

        ident_b = cb[:, CB_IDENT:CB_IDENT + 128]
        ident_f = cf[:, CF_IDENT:CF_IDENT + 128]

        def psb(i):
            return ps[:, i, :].bitcast(BF16)

        def ps2(i):
            return ps[:, i:i + 2, :].rearrange("p a b -> p (a b)")

        def slab_src(w_ap, c0, ncols=512):
            return w_ap.rearrange("(c p) n -> p c n", p=128)[:, :, c0:c0 + ncols]

        ring_i = [0]

        def load_slab(src_ap, slot=None):
            if slot is None:
                i = ring_i[0] % 4
                ring_i[0] += 1
            else:
                i = slot
            dma("pool", ring[i][:], src_ap, [], [Bring[i]], k_ring[i])
            return i

        def rmsnorm_front(src, srcB, t_col, xn_out, xn_B, on_pool=False, extra_w=()):
            c0 = t_col * 3
            Bst = Bstat[t_col]
            act(xn_out, src, AF.Square, [srcB], [xn_B, Bst] + list(extra_w), accum_out=stat[:, c0:c0 + 1])
            act(stat[:, c0 + 1:c0 + 2], stat[:, c0:c0 + 1], AF.Sqrt, [Bst, Bc], [Bst], scale=1.0 / D, bias=epsb[:])
            S.op("dve", lambda e: e.reciprocal(out=stat[:, c0 + 2:c0 + 3], in_=stat[:, c0 + 1:c0 + 2]), [Bst], [Bst])
            if on_pool:
                S.op("pool", lambda e: e.tensor_scalar(out=xn_out, in0=src, scalar1=stat[:, c0 + 2:c0 + 3], scalar2=1.0, op0=ALU.mult, op1=ALU.mult),
                     [srcB, Bst], [xn_B])
            else:
                ts(xn_out, src, stat[:, c0 + 2:c0 + 3], None, ALU.mult, None, [srcB, Bst], [xn_B])

        def TAP(name, src_ap, shape, dt, rd):
            if not debug:
                return
            d_ = nc.dram_tensor("dbg_" + name, shape, dt, kind="ExternalOutput").ap()
            kd = mkkey("dbg_" + name)
            dbg_keys.append(kd)
            dma("sp", d_, src_ap, rd, [], kd)

        dma("sp", cc[:], cc_d, [], [Bc], k_const)
        dma("sp", cf[:], cf_d, [], [Bcf], k_const)
        dma("sp", cb[:], cb_d, [], [Bc], k_const)
        Bcb2 = Buf("cb2")
        dma("sp", cb2[:], cb2_d, [], [Bcb2], k_const)
        dma("sp", pos_i[:], pos_d, [], [Bc], k_const)
        S.op("dve", lambda e: e.memset(epsb[:], EPS), [], [Bc])
        S.op("dve", lambda e: e.memset(onec[:], 1.0), [], [Bc])
        S.op("dve", lambda e: e.memset(zeroc[:], 0.0), [], [Bc])
        for j in range(2):
            act(scb[:, :, j], cc[:, CC_C:CC_C + 8], AF.Silu, [Bc], [Bmod])

        cp(pos_f[:], pos_i[:], [Bc], [Btab])
        MAGIC = 12582912.0
        invf_b = cf[:, CF_INVF:CF_INVF + 64]
        ang = cos_t
        for (dst, shift) in ((sin_t, 0.0), (cos_t, PI / 2)):
            a_ = dst[:]
            tt(a_, invf_b.unsqueeze(1).broadcast_to([128, NT, 64]), pos_f[:].unsqueeze(2).broadcast_to([128, NT, 64]),
               ALU.mult, [Bcf, Btab], [Btab])
            sc1 = xs[0][:, 0:NT * 64].rearrange("p (a b) -> p a b", b=64)
            ts(sc1, a_, shift, 1.0 / (2 * PI), ALU.add, ALU.mult, [Btab], [Bxs[0]])
            ts(sc1, sc1, MAGIC, None, ALU.add, None, [Bxs[0]], [Bxs[0]])
            ts(sc1, sc1, MAGIC, None, ALU.subtract, None, [Bxs[0]], [Bxs[0]])
            stt(a_, sc1, -6.28125, a_, ALU.mult, ALU.add, [Btab, Bxs[0]], [Btab])
            stt(a_, sc1, -0.0019352436065673828, a_, ALU.mult, ALU.add, [Btab, Bxs[0]], [Btab])
            stt(a_, sc1, -6.357301884918343e-08, a_, ALU.mult, ALU.add, [Btab, Bxs[0]], [Btab])
            if shift != 0.0:
                ts(a_, a_, shift, None, ALU.add, None, [Btab], [Btab])
            ts(a_, a_, PI, -PI, ALU.min, ALU.max, [Btab], [Btab])
            act(a_, a_, AF.Sin, [Btab], [Btab])

        ADA_ORDER = [2, 3, 0, 1, 4, 5, 6, 7, 8, 9, 10, 11]

        def ada_slab(sl, i=None, bank=5, evac=True):
            if i is None:
                i = load_slab(slab_src(adaw_d, sl * 512))
            lst = []
            for j in range(4):
                col = sl * 4 + j
                for k in range(8):
                    lst.append((ps[:, bank, 2 * col:2 * col + 2], ring[i][:, k, j * 128:(j + 1) * 128], scb[:, k, :], k == 0, k == 7))
            mms(lst, [Bring[i], Bmod], [Bps[bank]])
            if evac:
                ada_evac(sl, bank)

        def ada_evac(sl, bank=5):
            cp(modsb[:, sl * 4:sl * 4 + 4], ps[:, bank, 8 * sl:8 * sl + 8].rearrange("p (k two) -> p k two", two=2)[:, :, 0],
               [Bps[bank]], [Bmodsb])

        for sl in ADA_ORDER[:4]:
            ada_slab(sl, evac=False)
        k_tmp = mkkey("tmp")
        slots = [(xs[0], Bxs[0], k_xs[0]), (xs[1], Bxs[1], k_xs[1]), (tmp, Btmp, k_tmp)]
        def proj_tok(slab_i, t, bank, n=512):
            mms([(ps[:, bank, 0:n], hT[:, t // 4, k, (t % 4) * 128:(t % 4 + 1) * 128], ring[slab_i][:, k, 0:n], k == 0, k == 7) for k in range(8)],
                [BhT[t], Bring[slab_i]], [Bps[bank]])


        def a_front(t):
            x_t, x_B, x_k = slots[t % 3]
            dma("sp", x_t[:], x_d[t * 128:(t + 1) * 128, :], [Bring[3]], [x_B], x_k)
            rmsnorm_front(x_t[:], x_B, t, xn_all[:, t, :], Bxna[t], on_pool=True)

        def a_back(t):
            bk = 6 + (t % 2)
            pv = psb(bk)
            trs([(pv[:, c * 128:(c + 1) * 128], xn_all[:, t, c * 128:(c + 1) * 128]) for c in range(8)],
                [Bxna[t], Bc], [Bps[bk]], ident_b)
            evac_scaled(lambda c, t=t: hT[:, t // 4, c, (t % 4) * 128:(t % 4 + 1) * 128], pv, am, sm, [Bps[bk], Bams], [BhT[t]], n_act=2)

        def modcol(v):
            return modsb[:, 8 * v:8 * v + 8]

        a_front(0)
        a_front(1)
        for sl in ADA_ORDER[:4]:
            ada_evac(sl)
        stt(am[:], modcol(1), 1.0, cc[:, CC_ADAB + 8:CC_ADAB + 16], ALU.add, ALU.add, [Bmodsb, Bc], [Bams])
        tt(am[:], am[:], cc[:, CC_GPM:CC_GPM + 8], ALU.mult, [Bams, Bc], [Bams])
        tt(sm[:], modcol(0), cc[:, CC_ADAB:CC_ADAB + 8], ALU.add, [Bmodsb, Bc], [Bams])
        si_u = load_slab(slab_src(win_d, 0))

        def proj_u(t):
            bank = t % 4
            proj_tok(si_u, t, bank)
            cp(u_tok[:, t, :], ps[:, bank, :], [Bps[bank]], [Bu[t]])

        for t in range(2, NT):
            a_front(t)
            a_back(t - 2)
            if t >= 5:
                proj_u(t - 5)
        a_back(NT - 2)
        a_back(NT - 1)
        for t in range(NT - 5, NT):
            proj_u(t)

        TAP("hT", hT[:], [128, NB, 8, 512], BF16, BhT)
        TAP("am", am[:], [128, 8], F32, [Bams]); TAP("sm", sm[:], [128, 8], F32, [Bams])

        Mguard0 = Bxna
        for sl in (4, 5):
            ada_slab(sl)
        TAP("u_tok", u_tok[:], [128, NT, 512], BF16, Bu)

        dma("pool", poolw[:], poolw_d, [], [Bpoolw], k_misc)
        for gi in range(4):
            base = CB_POOLA + gi * 384
            for b in range(NB):
                bank = (gi * NB + b) % 4
                lst = []
                for tl in range(4):
                    t = b * 4 + tl
                    o_ = ps[:, bank, tl * 128:(tl + 1) * 128]
                    if t == 0:
                        lst.append((o_, u_tok[:, 0, gi * 128:(gi + 1) * 128], cb[:, base:base + 128], True, False))
                        lst.append((o_, u_tok[:, 0, gi * 128:(gi + 1) * 128], cb2[:, gi * 128:(gi + 1) * 128], False, True))
                    else:
                        lst.append((o_, u_tok[:, t, gi * 128:(gi + 1) * 128], cb[:, base + 128:base + 256], True, False))
                        lst.append((o_, u_tok[:, t - 1, gi * 128:(gi + 1) * 128], cb[:, base + 256:base + 384], False, True))
                rd = [Bu[t] for t in range(max(0, b * 4 - 1), b * 4 + 4)] + [Bc, Bcb2]
                mms(lst, rd, [Bps[bank]])
                act(pooledT[:, gi, b * 512:(b + 1) * 512], ps[:, bank, :], AF.Copy, [Bps[bank]], [Bpl[gi][b]] + (Mguard0 if (gi == 0 and b == 0) else []))
        for gi in range(4):
            for b in range(NB):
                bank = (gi * NB + b) % 4
                mms([(ps[:, bank, :], poolw[:, gi, :], pooledT[:, gi, b * 512:(b + 1) * 512], True, True)],
                    [Bpoolw, Bpl[gi][b]], [Bps[bank]])
                ts(mixedT[:, b, gi, :], ps[:, bank, :], cc[:, CC_PS + gi:CC_PS + gi + 1], None, ALU.mult, None,
                   [Bps[bank], Bc], [Bmx[gi][b]] + (Bu if (gi == 0 and b == 0) else []))

        si = load_slab(slab_src(win_d, 2048))
        for h in range(4):
            for b in range(NB):
                bank = (h * NB + b) % 4
                mms([(ps[:, bank, :], ring[si][:, k, h * 128:(h + 1) * 128], hT[:, b, k, :], k == 0, k == 7) for k in range(8)],
                    [Bring[si]] + BhT[b * 4:b * 4 + 4], [Bps[bank]])
                act(sgT[:, h, b * 512:(b + 1) * 512], ps[:, bank, :], AF.Silu, [Bps[bank]], [Bsg[h][b]])
        for sl in (6, 7):
            ada_slab(sl)

        Mguard1 = Bu + [x for r in Bpl for x in r]

        def finalize_mod():
            stt(af[:], modcol(4), 1.0, cc[:, CC_ADAB + 32:CC_ADAB + 40], ALU.add, ALU.add, [Bmodsb, Bc], [Bafs])
            tt(af[:], af[:], cc[:, CC_GPF:CC_GPF + 8], ALU.mult, [Bafs, Bc], [Bafs])
            tt(sf[:], modcol(3), cc[:, CC_ADAB + 24:CC_ADAB + 32], ALU.add, [Bmodsb, Bc], [Bafs])
            tt(cmc[:], modcol(2), cc[:, CC_ADAB + 16:CC_ADAB + 24], ALU.add, [Bmodsb, Bc], [Bcmc])
            tt(cmc[:], cmc[:], cc[:, CC_GQM:CC_GQM + 8], ALU.mult, [Bcmc, Bc], [Bcmc])
            tt(cfc[:], modcol(5), cc[:, CC_ADAB + 40:CC_ADAB + 48], ALU.add, [Bmodsb, Bc], [Bcmc])
            tt(cfc[:], cfc[:], cc[:, CC_GQF:CC_GQF + 8], ALU.mult, [Bcmc, Bc], [Bcmc])
            trs([(ps[0:8, 4, 0:128], cmc[:]), (ps[0:8, 4, 128:256], cfc[:])], [Bcmc, Bcf], [Bps[4]], ident_f)
            cp(crow[:], ps[0:8, 4, 0:256], [Bps[4]], [Bcrow, Bcb2])
            for v in range(2):
                dma("sp", scr_d[v:v + 1, :].rearrange("o (k p) -> (o k) p", p=128), crow[:, v * 128:(v + 1) * 128], [Bcrow], [Bscr[v]], k_scr[v])
            for v, dst in enumerate((coef_m, coef_f)):
                dma("sp", dst[:], scr_d[v:v + 1, :].broadcast_to([128, D]), [Bscr[v]], [Bcoef[v]], k_scr[v])


        TAP("mixedT", mixedT[:], [128, NB, 4, 512], BF16, [x for r in Bmx for x in r])
        TAP("sgT", sgT[:], [128, 4, S_TOK], BF16, [x for r in Bsg for x in r])
        TAP("cos", cos_t[:], [128, NT, 64], F32, [Btab]); TAP("sin", sin_t[:], [128, NT, 64], F32, [Btab])

        rsets = [(tmp, Btmp), (xs[1], Bxs[1])]
        i_q = load_slab(slab_src(win_d, 512))
        i_k = load_slab(slab_src(win_d, 1024))
        i_v = load_slab(slab_src(win_d, 1536))
        i_a8 = load_slab(slab_src(adaw_d, 8 * 512))
        dq4 = cf[:, CF_DQ:CF_DQ + 512].rearrange("p (h i) -> p h i", h=4)
        dvb4 = cf[:, CF_DV:CF_DV + 4].unsqueeze(2).broadcast_to([128, 4, 128])
        maskT = cf[:, CF_MASK:CF_MASK + 512]
        cdt = cf[:, CF_CD:CF_CD + 512]
        OB = [2, 3, 5]

        def qk_finish(T0, tt_, first):
            qi = tt_ % 4
            pq, pk = psb(6), psb(7)
            trs([(pq[:, hh * 128:(hh + 1) * 128], qrot[qi][:, hh * 128:(hh + 1) * 128]) for hh in range(4)], [Bqrot[qi], Bc], [Bps[6]], ident_b)
            trs([(pk[:, hh * 128:(hh + 1) * 128], k_tok[:, tt_, hh * 128:(hh + 1) * 128]) for hh in range(4)], [Bk[tt_], Bc], [Bps[7]], ident_b)
            tsl = slice(tt_ * 128, (tt_ + 1) * 128)
            tt(qT[:, :, tsl], pq[:, 0:512].rearrange("p (h i) -> p h i", h=4), dq4, ALU.mult, [Bps[6], Bcf], [BqT[tt_]] + first)
            for hh in range(4):
                act(kT[:, hh, tsl], pk[:, hh * 128:(hh + 1) * 128], AF.Identity, [Bps[7], Bc], [BkT[tt_]] + first,
                    scale=onec[:], bias=zeroc[:])

        def qk_pass(T0, guard):
            pend = []
            for tt_ in range(HT_):
                t = T0 + tt_
                qi = tt_ % 4
                for is_k in range(2):
                    bank = (2 * tt_ + is_k) % 4
                    proj_tok(i_k if is_k else i_q, t, bank, 512)
                    rs_t, rs_B = rsets[is_k]
                    r_ = [rs_t[:, i * 256:(i + 1) * 256].rearrange("p (g j) -> p g j", g=4) for i in range(4)]
                    pvw = ps[:, bank, :].rearrange("p (g two j) -> p g two j", g=4, two=2)
                    x1 = pvw[:, :, 0, :]; x2 = pvw[:, :, 1, :]
                    cb_ = cos_t[:, t, :].unsqueeze(1).broadcast_to([128, 4, 64])
                    sb_ = sin_t[:, t, :].unsqueeze(1).broadcast_to([128, 4, 64])
                    tt(r_[0], x1, cb_, ALU.mult, [Bps[bank], Btab], [rs_B])
                    tt(r_[1], x2, sb_, ALU.mult, [Bps[bank], Btab], [rs_B])
                    tt(r_[2], x2, cb_, ALU.mult, [Bps[bank], Btab], [rs_B])
                    tt(r_[3], x1, sb_, ALU.mult, [Bps[bank], Btab], [rs_B])
                    gd = guard if tt_ < 2 else []
                    if is_k:
                        dst = k_tok[:, tt_, :].rearrange("p (h two j) -> p h two j", h=4, two=2)
                        wr = [Bk[tt_]] + gd
                    else:
                        dst = qrot[qi][:].rearrange("p (h two j) -> p h two j", h=4, two=2)
                        wr = [Bqrot[qi]]
                    S.op("pool", (lambda e, dst=dst, r_=r_: e.tensor_tensor(out=dst[:, :, 0, :], in0=r_[0], in1=r_[1], op=ALU.subtract)), [rs_B], wr)
                    S.op("pool", (lambda e, dst=dst, r_=r_: e.tensor_tensor(out=dst[:, :, 1, :], in0=r_[2], in1=r_[3], op=ALU.add)), [rs_B], wr)
                pend.append(tt_)
                if len(pend) > 2:
                    tp = pend.pop(0)
                    qk_finish(T0, tp, guard if tp < 2 else [])
            while pend:
                tp = pend.pop(0)
                qk_finish(T0, tp, guard if tp < 2 else [])

        def v_pass(T0, guard):
            for tt_ in range(HT_):
                bank = tt_ % 4
                proj_tok(i_v, T0 + tt_, bank, 512)
                tt(v_tok[:, tt_, :].rearrange("p (h e) -> p h e", h=4), ps[:, bank, :].rearrange("p (h e) -> p h e", h=4), dvb4,
                   ALU.mult, [Bps[bank], Bcf], [Bv[tt_]] + (guard if tt_ < 2 else []))

        def retention_pass(T0, hooks=None):
            RG = [Bxs[0]]
            ok = lambda T: T0 <= T < T0 + HT_

            def a1(T):
                tt_ = T - T0; p2 = T % 2
                tsl = slice(tt_ * 128, (tt_ + 1) * 128)
                mms([(ps[:, p2, hh * 128:(hh + 1) * 128], kT[:, hh, tsl], qT[:, hh, tsl], True, True) for hh in range(4)],
                    [BkT[tt_], BqT[tt_]], [Bps[p2]])
                tt(sTm[p2][:], ps[:, p2, :], maskT, ALU.mult, [Bps[p2], Bcf], [BsTm[p2]])

            def a2_pe(T):
                tt_ = T - T0; p2 = T % 2
                tsl = slice(tt_ * 128, (tt_ + 1) * 128)
                b_o = OB[T % 3]
                lst = []
                for hh in range(4):
                    hs = slice(hh * 128, (hh + 1) * 128)
                    lst.append((ps[:, b_o, hs], sTm[p2][:, hs], v_tok[:, tt_, hs], True, T == 0))
                    if T > 0:
                        lst.append((ps[:, b_o, hs], qT[:, hh, tsl], Sbf[(T - 1) % 2][:, hs], False, True))
                mms(lst, [BsTm[p2], Bv[tt_], BqT[tt_]] + ([BSbf[(T - 1) % 2]] if T > 0 else []), [Bps[b_o]])
                if T < NT - 1:
                    mms([(ps[:, 4, hh * 128:(hh + 1) * 128], k_tok[:, tt_, hh * 128:(hh + 1) * 128], v_tok[:, tt_, hh * 128:(hh + 1) * 128], True, True)
                         for hh in range(4)], [Bk[tt_], Bv[tt_]], [Bps[4]])

            def st1(T):
                if 0 < T < NT - 1:
                    tt(Rst[:], Rst[:], cdt, ALU.mult, [BR, Bcf], [BR])

            def st2(T):
                if T >= NT - 1:
                    return
                p2 = T % 2
                if T == 0:
                    cp(Rst[:], ps[:, 4, :], [Bps[4]], [BR] + RG)
                else:
                    tt(Rst[:], Rst[:], ps[:, 4, :], ALU.add, [BR, Bps[4]], [BR])
                for hh in range(4):
                    act(Sbf[p2][:, hh * 128:(hh + 1) * 128], Rst[:, hh * 128:(hh + 1) * 128], AF.Copy, [BR], [BSbf[p2]],
                        scale=float((1.0 - 2.0 ** (-5.0 - hh)) ** 128))

            def b_stats(T):
                p2 = T % 2; b_o = OB[T % 3]
                for hh in range(4):
                    S.op("dve", (lambda e, hh=hh: e.bn_stats(out=gst[:, p2, hh, :], in_=ps[:, b_o, hh * 128:(hh + 1) * 128])),
                         [Bps[b_o]], [Bgst[p2]])

            def b_aggr(T):
                p2 = T % 2
                for hh in range(4):
                    S.op("dve", (lambda e, hh=hh: e.bn_aggr(out=gmv[:, p2, hh, :], in_=gst[:, p2, hh, :])), [Bgst[p2]], [Bgmv[p2]])
                act(gsq[:, p2, :], gmv[:, p2, :, 1], AF.Sqrt, [Bgmv[p2], Bc], [Bgsq[p2]], bias=epsb[:])

            def b_recip(T):
                p2 = T % 2
                S.op("dve", (lambda e: e.reciprocal(out=grs[:, p2, :], in_=gsq[:, p2, :])), [Bgsq[p2]], [Bgrs[p2]])

            def b_norm(T):
                p2 = T % 2; b_o = OB[T % 3]
                stt(gnb[:, p2, :], gmv[:, p2, :, 0], -1.0, grs[:, p2, :], ALU.mult, ALU.mult, [Bgmv[p2], Bgrs[p2]], [Bgnb[p2]])
                for hh in range(4):
                    act(onb[p2][:, hh * 128:(hh + 1) * 128], ps[:, b_o, hh * 128:(hh + 1) * 128], AF.Identity,
                        [Bps[b_o], Bgrs[p2], Bgnb[p2]], [Bon[p2]] + (RG if T < 2 else []), scale=grs[:, p2, hh:hh + 1], bias=gnb[:, p2, hh:hh + 1])

            def c_tr(T):
                p2 = T % 2; b_t = 6 + p2
                pv = psb(b_t)
                trs([(pv[:, hh * 128:(hh + 1) * 128], onb[p2][:, hh * 128:(hh + 1) * 128]) for hh in range(4)], [Bon[p2], Bc], [Bps[b_t]], ident_b)

            def c_out(T):
                p2 = T % 2; b_t = 6 + p2
                pv = psb(b_t)
                tt(retoutT[:, T // 4, :, (T % 4) * 128:(T % 4 + 1) * 128], pv[:, 0:512].rearrange("p (h i) -> p h i", h=4),
                   sgT[:, :, T * 128:(T + 1) * 128], ALU.mult,
                   [Bps[b_t]] + [Bsg[hh][T // 4] for hh in range(4)], [Bro[T]])

            a1(T0)
            for i in range(T0, T0 + HT_ + 3):
                if ok(i + 1): a1(i + 1)
                if ok(i - 1): b_stats(i - 1)
                if ok(i):
                    a2_pe(i)
                    st1(i)
                if ok(i - 3): c_tr(i - 3)
                if ok(i - 2): b_recip(i - 2)
                if ok(i):
                    st2(i)
                    if hooks and (i - T0) in hooks:
                        hooks[i - T0]()
                if ok(i - 1): b_aggr(i - 1)
                if ok(i - 2): b_norm(i - 2)
                if ok(i - 3): c_out(i - 3)

        def taps_qkv(ps_):
            TAP(f"qT{ps_}", qT[:], [128, 4, HT_ * 128], BF16, BqT[:HT_]); TAP(f"kT{ps_}", kT[:], [128, 4, HT_ * 128], BF16, BkT[:HT_])
            TAP(f"k_tok{ps_}", k_tok[:], [128, HT_, 512], BF16, Bk[:HT_]); TAP(f"v_tok{ps_}", v_tok[:], [128, HT_, 512], BF16, Bv[:HT_])

        def ada_step(sl):
            def fn():
                ada_slab(sl, i_a8, bank=4)
                if sl < 11:
                    load_slab(slab_src(adaw_d, (sl + 1) * 512), slot=i_a8)
            return fn
        qk_pass(0, Mguard1)
        v_pass(0, Mguard1)
        taps_qkv(0)
        retention_pass(0, {0: ada_step(8), 2: ada_step(9), 4: ada_step(10), 6: ada_step(11)})
        qk_pass(HT_, [])
        v_pass(HT_, [])
        taps_qkv(1)
        SGall = [x for r in Bsg for x in r]
        gsl = [[ring[0], gate_sg[0]], [ring[1], gate_sg[1]]]
        Bgs = [[Bring[0], Buf("gate_sg0")], [Bring[1], Buf("gate_sg1")]]
        k_gs = [[k_ring[0], mkkey("gsg0")], [k_ring[1], mkkey("gsg1")]]
        for g in range(2):
            dma("pool", gsl[g][0][:], slab_src(win_d, 2560 + g * 1024), [], [Bgs[g][0]], k_gs[g][0])
        for wi, (w_s, w_dd) in enumerate(((wbp_s, wbp_d), (wbr_s, wbr_d))):
            dma("pool", w_s[:].rearrange("p (c a) n -> p c (a n)", a=2), w_dd.rearrange("(c p) n -> p c n", p=128),
                [], [Bwb[wi], Bring[2 + wi]], k_wb[wi])
        retention_pass(HT_)
        finalize_mod()
        TAP("coef_m", coef_m[:], [128, D], F32, [Bcoef[0]]); TAP("coef_f", coef_f[:], [128, D], F32, [Bcoef[1]])
        TAP("af", af[:], [128, 8], F32, [Bafs]); TAP("sf", sf[:], [128, 8], F32, [Bafs])
        TAP("retoutT", retoutT[:], [128, NB, 4, 512], BF16, Bro)

        for g in range(2):
            dma("pool", gsl[g][1][:], slab_src(win_d, 2560 + g * 1024 + 512), [], [Bgs[g][1]] + SGall, k_gs[g][1])
        WOguard = BsTm + BSbf + Bon + [BR, Btab] + Bqrot
        for i in range(2):
            dma("pool", wo_slab[i][:], slab_src(wout_d, i * 512), [], [Bwo[i]] + WOguard, k_wo[i])

        def post_norm_residual(t, ybank, coef, coefB, src_ap, srcB_list, dst_ap, dstB_list, col):
            sl = t % 2
            y = ps2(ybank)
            c0 = col * 3
            Bst = Bstat[col]
            act(junk[:], y, AF.Square, [Bps[ybank], Bps[ybank + 1]], [Bjunk, Bst], accum_out=stat[:, c0:c0 + 1])
            act(stat[:, c0 + 1:c0 + 2], stat[:, c0:c0 + 1], AF.Sqrt, [Bst, Bc], [Bst], scale=1.0 / D, bias=epsb[:])
            S.op("dve", lambda e: e.reciprocal(out=stat[:, c0 + 2:c0 + 3], in_=stat[:, c0 + 1:c0 + 2]), [Bst], [Bst])
            dma("sp", xs[sl][:], src_ap, srcB_list, [Bxs[sl]] + ([BR] + Bon if (t == 0 and coef is coef_m) else []), k_xs[sl])
            stt(tmp[:], y, stat[:, c0 + 2:c0 + 3], coef[:], ALU.mult, ALU.mult, [Bps[ybank], Bps[ybank + 1], Bst, coefB], [Btmp])
            tt(xs[sl][:], xs[sl][:], tmp[:], ALU.add, [Bxs[sl], Btmp], [Bxs[sl]])
            dma("sp", dst_ap, xs[sl][:], [Bxs[sl]], dstB_list, k_xs[sl])

        Eguard = [Bcf, Bpoolw]
        MguardE = BqT + BkT + Bk + Bv
        unit = 0
        h2T_early = at("h2T_early", MB + 16 * KB, [128, 8, 512], BF16)
        Bh2e = Buf("h2T_early")
        stE = at("stE", MB + 8 * KB, [128, D], F32)
        xnE = [at(f"xnE{i}", MB + 12 * KB + i * 2 * KB, [128, D], BF16) for i in range(2)]
        Eg1 = [Bmg[c][1] for c in range(8)]
        BstE = Buf("stE"); BxnE = [Buf("xnE0"), Buf("xnE1")]
        k_stE = mkkey("stE")
        XN0 = [(xn[0], Bxn[0]), (xn[1], Bxn[1]), (xnE[0], BxnE[0]), (xnE[1], BxnE[1])]

        def early_prep_back(tl, bk):
            pv = psb(bk)
            trs([(pv[:, c * 128:(c + 1) * 128], XN0[tl][0][:, c * 128:(c + 1) * 128]) for c in range(8)], [XN0[tl][1], Bc], [Bps[bk]], ident_b)
            evac_scaled(lambda c: h2T_early[:, c, tl * 128:(tl + 1) * 128], pv, af, sf, [Bps[bk], Bafs],
                        [Bh2e] + ([Bmg[c][2] for c in range(8)] if tl == 0 else []))

        def early_load(tl):
            dma("sp", stE[:], x1_d[tl * 128:(tl + 1) * 128, :], [Bx1d[tl]], [BstE] + (Eg1 if tl == 0 else []), k_stE)

        ucnt = [0]

        def e_units(b, dcs):
            rdh = BhT[b * 4:b * 4 + 4]
            for dc in dcs:
                unit = ucnt[0]
                a_, off = dc // 4, (dc % 4) * 128
                bb = (unit % 2) * 4
                e0 = (unit % 2) * 4
                bks = [0, 1, 2, 3] if bb == 0 else [4, 5, 6, 7]
                for g in range(2):
                    mms([(ps[:, bks[g], :], gsl[g][a_][:, k, off:off + 128], hT[:, b, k, :], k == 0, k == 7) for k in range(8)],
                        [Bgs[g][a_]] + rdh, [Bps[bks[g]]])
                mms([(ps[:, bks[2], :], wbp_s[:, gi * 2 + a_, off:off + 128], mixedT[:, b, gi, :], gi == 0, gi == 3) for gi in range(4)],
                    [Bwb[0]] + [Bmx[gi][b] for gi in range(4)], [Bps[bks[2]]])
                mms([(ps[:, bks[3], :], wbr_s[:, h * 2 + a_, off:off + 128], retoutT[:, b, h, :], h == 0, h == 3) for h in range(4)],
                    [Bwb[1]] + Bro[b * 4:b * 4 + 4], [Bps[bks[3]]])
                gd = Eguard if unit < 2 else []
                for g in range(2):
                    act(esc[e0 + g][:], ps[:, bks[g], :], AF.Sigmoid, [Bps[bks[g]], Bc], [Besc[e0 + g]] + gd,
                        bias=cc[:, CC_BG + g * 8 + dc:CC_BG + g * 8 + dc + 1])
                tt(esc[e0 + 2][:], esc[e0][:], ps[:, bks[2], :], ALU.mult, [Besc[e0], Bps[bks[2]]], [Besc[e0 + 2]] + gd)
                tt(esc[e0 + 3][:], esc[e0 + 1][:], ps[:, bks[3], :], ALU.mult, [Besc[e0 + 1], Bps[bks[3]]], [Besc[e0 + 3]] + gd)
                tt(mergedT[:, b, dc, :], esc[e0 + 2][:], esc[e0 + 3][:], ALU.add, [Besc[e0 + 2], Besc[e0 + 3]],
                   [Bmg[dc][b]] + (MguardE if unit == 0 else []))
                ucnt[0] += 1
                if b == 3 and 2 <= dc <= 5:
                    tl_ = dc - 2
                    rmsnorm_front(stE[:], BstE, 4 + tl_, XN0[tl_][0][:], XN0[tl_][1], extra_w=(Eg1 if tl_ >= 2 else ()))
                    if tl_ < 3:
                        early_load(tl_ + 1)

        def f_tiles(b):
            for tl in range(4):
                t = b * 4 + tl
                yb = tl * 2 if b < 3 else (tl % 2) * 2
                lst = []
                for half in range(2):
                    for c in range(8):
                        lst.append((ps[:, yb + half, :], mergedT[:, b, c, tl * 128:(tl + 1) * 128], wo_slab[half][:, c, :], c == 0, c == 7))
                mms(lst, [Bmg[c][b] for c in range(8)] + Bwo, [Bps[yb], Bps[yb + 1]])
                post_norm_residual(t, yb, coef_m, Bcoef[0], x_d[t * 128:(t + 1) * 128, :], [], x1_d[t * 128:(t + 1) * 128, :], [Bx1d[t]], t % 4)
            TAP(f"mergedT{b}", mergedT[:, b], [128, 8, 512], BF16, [Bmg[c][b] for c in range(8)])

        def w2_load(i_, gl):
            dma("pool", w2[i_][:], wff2_d[i_ * 512:(i_ + 1) * 512, :].rearrange("(c p) n -> p c n", p=128), [], [Bw2[i_]] + gl, k_w2[i_])

        def e_prefetch(b):
            if b < 3:
                dma("pool", w1[b][:], slab_src(wff1_d, b * 512), [], [Bw1[b]] + BhT[b * 4:b * 4 + 4], k_w1[b])
            if b == 0:
                w2_load(4, [Bmg[c][0] for c in range(8)])
            if b == 1:
                w2_load(0, [Bmx[gi][bb_] for gi in range(4) for bb_ in (0, 1)])
                w2_load(2, Bro[0:8])

        e_units(0, range(8))
        for b in range(1, NB):
            if b == 3:
                early_load(0)
            e_units(b, range(0, 4))
            f_tiles(b - 1)
            e_prefetch(b - 1)
            e_units(b, range(4, 8))
        for tl in range(4):
            early_prep_back(tl, 6 + (tl % 2))
        f_tiles(3)
        e_prefetch(3)
        def prep_front(b, tl):
            t = b * 4 + tl
            sl = t % 2
            dma("sp", xs[sl][:], x1_d[t * 128:(t + 1) * 128, :], [Bx1d[t]], [Bxs[sl]], k_xs[sl])
            rmsnorm_front(xs[sl][:], Bxs[sl], 4 + tl, xn[sl][:], Bxn[sl])

        H2guard = Bwo + BsTm + BSbf + Bon + [BR]
        HIDguard = Bring + Bwb
        def prep_back(b, tl, xn_t=None, xn_B=None):
            hb = b % 2
            t = b * 4 + tl
            sl = t % 2
            if xn_t is None:
                xn_t, xn_B = xn[sl], Bxn[sl]
            bk = 6 + (t % 2)
            pv = psb(bk)
            trs([(pv[:, c * 128:(c + 1) * 128], xn_t[:, c * 128:(c + 1) * 128]) for c in range(8)], [xn_B, Bc], [Bps[bk]], ident_b)
            evac_scaled(lambda c: h2T[hb][:, c, tl * 128:(tl + 1) * 128], pv, af, sf, [Bps[bk], Bafs],
                        [Bh2[hb]] + (H2guard if b <= 2 else []))

        def ff1(b):
            hb = b % 2
            h_t, h_B = (h2T_early, Bh2e) if b == 0 else (h2T[hb], Bh2[hb])
            for f in range(32):
                bank = 4 + (f % 2)
                s_, fo = f // 4, (f % 4) * 128
                mms([(ps[:, bank, :], w1[s_][:, k, fo:fo + 128], h_t[:, k, :], k == 0, k == 7) for k in range(8)],
                    [Bw1[s_], h_B], [Bps[bank]])
                r = f % 2
                act(relu_sc[r][:], ps[:, bank, :], AF.Relu, [Bps[bank]], [Brelu[r]] + ([Bcoef[0]] if (b == 0 and f < 2) else []))
                tt(hidden[:, f, :], relu_sc[r][:], relu_sc[r][:], ALU.mult, [Brelu[r]], [Bhid[f]] + (HIDguard if b == 0 else []))

        def ff2_tile(b, tl):
            t = b * 4 + tl
            yb = (t % 2) * 2
            lst = []
            forder = [f for sl_ in (0, 2, 4, 1, 3, 7, 5, 6) for f in range(sl_ * 4, sl_ * 4 + 4)]
            for half in range(2):
                for n_, f in enumerate(forder):
                    lst.append((ps[:, yb + half, :], hidden[:, f, tl * 128:(tl + 1) * 128], w2[f // 4][:, f % 4, half * 512:(half + 1) * 512], n_ == 0, n_ == 31))
            mms(lst, Bhid + Bw2, [Bps[yb], Bps[yb + 1]])
            post_norm_residual(t, yb, coef_f, Bcoef[1], x1_d[t * 128:(t + 1) * 128, :], [Bx1d[t]], out_d[t * 128:(t + 1) * 128, :], [], 8 + (t % 4))

        chain = [Buf("wchain0"), Buf("wchain1")]
        nch = [0]

        def chained(fn_dma, wbuf_list):
            c = chain[nch[0] % 2]
            nch[0] += 1
            fn_dma(Bxs if nch[0] <= 2 else [], wbuf_list + [c])
        chained(lambda rd, wr: dma("pool", w1[3][:], slab_src(wff1_d, 3 * 512), rd, wr, k_w1[3]), [Bw1[3]] + BhT[12:16])
        for i in (4, 5):
            chained(lambda rd, wr, i=i: dma("pool", w1[i][:], slab_src(wff1_d, i * 512), rd, wr, k_w1[i]), [Bw1[i]] + SGall + [Bgs[0][1], Bgs[1][1]])
        for i in (6, 7):
            chained(lambda rd, wr, i=i: dma("pool", w1[i][:], slab_src(wff1_d, i * 512), [], wr, k_w1[i]), [Bw1[i]] + Besc + [Bcf, Bpoolw])
        for (i_, gl) in ((1, [Bmx[gi][bb_] for gi in range(4) for bb_ in (2, 3)]), (3, Bro[8:16]), (7, [Bmg[c][3] for c in range(8)]),
                         (5, Eg1 + [BstE] + BxnE)):
            chained(lambda rd, wr, i_=i_: dma("pool", w2[i_][:], wff2_d[i_ * 512:(i_ + 1) * 512, :].rearrange("(c p) n -> p c n", p=128), [], wr, k_w2[i_]),
                    [Bw2[i_]] + gl)

        for b in range(NB):
            ff1(b)
            if b == 0:
                TAP("hidden", hidden[:], [128, 32, 512], BF16, Bhid)
                TAP("h2T", h2T_early[:], [128, 8, 512], BF16, [Bh2e])
                w2_load(6, [Bmg[c][2] for c in range(8)] + [Bh2e])
            nxt = b + 1 < NB
            if nxt:
                prep_front(b + 1, 0)
                prep_front(b + 1, 1)
            ff2_tile(b, 0)
            if nxt:
                prep_back(b + 1, 0)
                prep_front(b + 1, 2)
            ff2_tile(b, 1)
            if nxt:
                prep_back(b + 1, 1)
                prep_front(b + 1, 3)
            ff2_tile(b, 2)
            if nxt:
                prep_back(b + 1, 2)
                prep_back(b + 1, 3)
            ff2_tile(b, 3)

        with nc.Block() as block:
            S.emit(block, sems, final_wait_keys=[k_xs[0], k_xs[1]] + dbg_keys)
    return nc


_NC_CACHE = {}


def _col(v, n):
    return np.ascontiguousarray(np.asarray(v, np.float32).reshape(n, 128).T)


def make_in_maps(x, c, positions, ada_w, ada_b, mix_pre_g, mix_post_g, ffn_pre_g, ffn_post_g,
                 w_in, b_branch_gate, pool_w, pool_scale, w_branch_pool, w_branch_ret, w_out, w_ff1, w_ff2):
    cf, cb, cb2 = _consts()
    f = lambda a: np.ascontiguousarray(np.asarray(a, np.float32))
    shared = {
        "cf": cf, "cb": cb, "cb2": cb2,
        "ada_w": f(ada_w[0]), "w_in": f(w_in[0]),
        "pool_w": np.ascontiguousarray(np.asarray(pool_w[0], np.float32).transpose(1, 0, 2)),
        "w_bp": f(w_branch_pool[0]), "w_br": f(w_branch_ret[0]), "w_out": f(w_out[0]),
        "w_ff1": f(w_ff1[0]), "w_ff2": f(w_ff2[0]),
    }
    maps = []
    for b in range(8):
        cc = np.zeros((128, CC_N), np.float32)
        cc[:, CC_C:CC_C + 8] = _col(c[b], 8)
        cc[:, CC_ADAB:CC_ADAB + 48] = _col(ada_b[0], 48)
        cc[:, CC_GPM:CC_GPM + 8] = _col(mix_pre_g[0], 8)
        cc[:, CC_GQM:CC_GQM + 8] = _col(mix_post_g[0], 8)
        cc[:, CC_GPF:CC_GPF + 8] = _col(ffn_pre_g[0], 8)
        cc[:, CC_GQF:CC_GQF + 8] = _col(ffn_post_g[0], 8)
        cc[:, CC_BG:CC_BG + 16] = _col(b_branch_gate[0], 16)
        cc[:, CC_PS:CC_PS + 4] = _col(pool_scale[0], 4)
        m = dict(shared)
        m["x"] = f(x[b])
        m["pos"] = np.ascontiguousarray(np.asarray(positions[b], np.int32).reshape(NT, 128).T)
        m["cc"] = cc
        maps.append(m)
    return maps


def kernel(**inputs):
    if "nc" not in _NC_CACHE:
        _NC_CACHE["nc"] = build()
    nc = _NC_CACHE["nc"]
    maps = make_in_maps(**inputs)
    res = run_bass_kernel_spmd(nc, maps, core_ids=list(range(8)))
    return np.stack([np.asarray(r["out"], np.float32) for r in res.results], axis=0)
```

```python
import contextlib
import numpy as np
import ml_dtypes
import concourse.bass as bass
import concourse.mybir as mybir
from concourse.bass_utils import run_bass_kernel_spmd

F32 = mybir.dt.float32
BF16 = mybir.dt.bfloat16
I32 = mybir.dt.int32
U8 = mybir.dt.uint8
AF = mybir.ActivationFunctionType
ALU = mybir.AluOpType
PI = float(np.pi)
EPS = 1e-6
S_TOK = 2048
D = 1024
NT = 16
NB = 4
KB = 1024


class Buf:
    __slots__ = ("name", "last_write", "reads")

    def __init__(self, name):
        self.name = name
        self.last_write = None
        self.reads = []


class DmaKey:
    def __init__(self, name, wait_total=False):
        self.name = name
        self.count = 0
        self.wait_total = wait_total
        self.sem = None


class Op:
    __slots__ = ("eng", "fn", "deps", "signal", "seq", "dma_key", "dma_count", "name")


class Sched:
    ENGS = ("pe", "act", "dve", "pool", "sp")

    def __init__(self, same_engine_sync=()):
        self.ops = []
        self.same_engine_sync = set(same_engine_sync)
        self.keys = []

    def key(self, name, wait_total=False):
        k = DmaKey(name, wait_total)
        self.keys.append(k)
        return k

    def op(self, eng, fn, reads=(), writes=(), dma_key=None, name=""):
        o = Op()
        o.eng = eng; o.fn = fn; o.signal = False; o.seq = None
        o.dma_key = dma_key; o.name = name
        if dma_key is not None:
            dma_key.count += 16
            o.dma_count = dma_key.count
        else:
            o.dma_count = None
        deps = []
        for b in reads:
            if b.last_write is not None:
                deps.append(b.last_write)
        for b in writes:
            if b.last_write is not None:
                deps.append(b.last_write)
            deps.extend(b.reads)
        fd = []
        seen = set()
        for d in deps:
            if id(d) in seen or d is o:
                continue
            seen.add(id(d))
            if d.dma_key is None and d.eng == eng and eng not in self.same_engine_sync:
                continue
            if d.dma_key is not None and d.dma_key is dma_key and dma_key.wait_total:
                continue
            fd.append(d)
        o.deps = fd
        for d in fd:
            if d.dma_key is None:
                d.signal = True
        for b in reads:
            b.reads.append(o)
        for b in writes:
            b.last_write = o
            b.reads = []
        self.ops.append(o)
        return o

    def emit(self, block, sems, final_wait_keys=()):
        cnt = {e: 0 for e in self.ENGS}
        for o in self.ops:
            if o.dma_key is None and o.signal:
                cnt[o.eng] += 1
                o.seq = cnt[o.eng]
        by_eng = {e: [o for o in self.ops if o.eng == e] for e in self.ENGS}

        def run(eng_name, eng):
            waited = {}
            for o in by_eng[eng_name]:
                for d in o.deps:
                    if d.dma_key is not None:
                        k = d.dma_key
                        val = k.count if k.wait_total else d.dma_count
                        kk = ("k", id(k))
                        if waited.get(kk, 0) >= val:
                            continue
                        waited[kk] = val
                        eng.wait_ge(k.sem, val)
                    else:
                        kk = ("e", d.eng)
                        if waited.get(kk, 0) >= d.seq:
                            continue
                        waited[kk] = d.seq
                        eng.wait_ge(sems[d.eng], d.seq)
                ins = o.fn(eng)
                if o.dma_key is not None:
                    ins.then_inc(o.dma_key.sem, 16)
                elif o.signal:
                    ins.then_inc(sems[eng_name], 1)
            if eng_name == "sp":
                for k in final_wait_keys:
                    eng.wait_ge(k.sem, k.count)

        @block.tensor
        def _(e):
            run("pe", e)

        @block.scalar
        def _(e):
            run("act", e)

        @block.vector
        def _(e):
            run("dve", e)

        @block.gpsimd
        def _(e):
            run("pool", e)

        @block.sync
        def _(e):
            run("sp", e)


POOL_WINDOWS = (2, 4, 8, 16)
CF_INVF, CF_DQ, CF_DV, CF_CD, CF_IDENT, CF_MASK = 0, 64, 576, 580, 1092, 1220
CF_N = 1732
CB_IDENT, CB_POOLA = 0, 128
CB_N = 128 + 4 * 3 * 128
CC_C, CC_ADAB, CC_GPM, CC_GQM, CC_GPF, CC_GQF, CC_BG, CC_PS = 0, 8, 56, 64, 72, 80, 88, 104
CC_N = 108


def _consts():
    cf = np.zeros((128, CF_N), np.float32)
    invf = (10000.0 ** (-(np.arange(64, dtype=np.float64) / 64.0))).astype(np.float32)
    cf[:, CF_INVF:CF_INVF + 64] = invf[None, :]
    gam = 1.0 - 2.0 ** (-5.0 - np.arange(4, dtype=np.float64))
    il = np.arange(128, dtype=np.float64)
    for h in range(4):
        cf[:, CF_DQ + h * 128:CF_DQ + (h + 1) * 128] = (gam[h] ** (il + 1) / np.sqrt(128.0))[None, :]
        cf[:, CF_DV + h] = gam[h] ** (-(il + 1))
        cf[:, CF_CD + h * 128:CF_CD + (h + 1) * 128] = gam[h] ** 128
        j = il[:, None]; i = il[None, :]
        same = (np.floor(j / 64) == np.floor(i / 64))
        m = np.where(j <= i, 1.0, np.where(same, gam[h] ** (2 * (j - i)), 0.0))
        cf[:, CF_MASK + h * 128:CF_MASK + (h + 1) * 128] = m
    cf[:, CF_IDENT:CF_IDENT + 128] = np.eye(128)
    cb = np.zeros((128, CB_N), np.float32)
    cb[:, CB_IDENT:CB_IDENT + 128] = np.eye(128)
    s = np.arange(128)[:, None]; t = np.arange(128)[None, :]
    for gi, w in enumerate(POOL_WINDOWS):
        cnt0 = np.minimum(t + 1, w).astype(np.float64)
        d0 = np.where((s <= t) & (s > t - w), 1.0 / cnt0, 0.0) - (s == t)
        dg = np.where((s <= t) & (s > t - w), 1.0 / w, 0.0) - (s == t)
        off = np.where(s > 128 + t - w, 1.0 / w, 0.0)
        base = CB_POOLA + gi * 3 * 128
        cb[:, base:base + 128] = d0
        cb[:, base + 128:base + 256] = dg
        cb[:, base + 256:base + 384] = off
    cb_bf = cb.astype(ml_dtypes.bfloat16)
    cb2 = np.zeros((128, 512), np.float32)
    for gi in range(4):
        base = CB_POOLA + gi * 3 * 128
        hi = cb_bf[:, base:base + 128].astype(np.float32)
        cb2[:, gi * 128:(gi + 1) * 128] = cb[:, base:base + 128] - hi
    return cf, cb_bf, cb2.astype(ml_dtypes.bfloat16)


def build(debug=False):
    nc = bass.Bass("TRN2", target_bir_lowering=False, dynamic_dma_scratch_size=8192)

    def din(name, shape, dt=F32):
        return nc.dram_tensor(name, shape, dt, kind="ExternalInput").ap()

    x_d = din("x", [S_TOK, D])
    pos_d = din("pos", [128, NT], I32)
    cc_d = din("cc", [128, CC_N])
    cf_d = din("cf", [128, CF_N])
    cb_d = din("cb", [128, CB_N], BF16)
    cb2_d = din("cb2", [128, 512], BF16)
    adaw_d = din("ada_w", [D, 6 * D])
    win_d = din("w_in", [D, 4608])
    poolw_d = din("pool_w", [128, 4, 128])
    wbp_d = din("w_bp", [512, D])
    wbr_d = din("w_br", [512, D])
    wout_d = din("w_out", [D, D])
    wff1_d = din("w_ff1", [D, 4 * D])
    wff2_d = din("w_ff2", [4 * D, D])
    out_d = nc.dram_tensor("out", [S_TOK, D], F32, kind="ExternalOutput").ap()
    x1_d = nc.dram_tensor("x1d", [S_TOK, D], F32, kind="ExternalOutput" if debug else "Internal").ap()
    scr_d = nc.dram_tensor("scr", [2, D], F32, kind="Internal").ap()
    dbg = {}

    S = Sched(same_engine_sync=("act", "dve", "pool"))
    es = contextlib.ExitStack()
    with es:
        ARENA = 208 * KB
        arena = es.enter_context(nc.sbuf_tensor("arena", [128, ARENA], U8))
        a0 = nc.sbuf_base - ARENA

        def at(name, off, shape, dt):
            return nc.alloc_sbuf_tensor_at(name, shape, dt, offset=a0 + off)

        o = 0
        def take(name, shape, dt, nbytes):
            nonlocal o
            t_ = at(name, o, shape, dt)
            o += (nbytes + 31) // 32 * 32
            return t_
        cc = take("cc", [128, CC_N], F32, CC_N * 4)
        cb = take("cb", [128, CB_N], BF16, CB_N * 2)
        pos_i = take("pos_i", [128, NT], I32, 64)
        pos_f = take("pos_f", [128, NT], F32, 64)
        am = take("am", [128, 8], F32, 32); sm = take("sm", [128, 8], F32, 32)
        af = take("af", [128, 8], F32, 32); sf = take("sf", [128, 8], F32, 32)
        cmc = take("cmc", [128, 8], F32, 32); cfc = take("cfc", [128, 8], F32, 32)
        scb = take("scb", [128, 8, 2], BF16, 32)
        epsb = take("epsb", [128, 1], F32, 32)
        onec = take("onec", [128, 1], F32, 32)
        zeroc = take("zeroc", [128, 1], F32, 32)
        stat = take("stat", [128, 64], F32, 256)
        gst = take("gst", [128, 2, 4, 6], F32, 192)
        gmv = take("gmv", [128, 2, 4, 2], F32, 64)
        grs = take("grs", [128, 2, 4], F32, 32)
        gsq = take("gsq", [128, 2, 4], F32, 32)
        gnb = take("gnb", [128, 2, 4], F32, 32)
        crow = take("crow", [8, 256], F32, 1024)
        modsb = take("modsb", [128, 48], F32, 192)
        cb2 = at("cb2", o - 1024 - 192, [128, 512], BF16)
        assert o <= 6 * KB, o
        coef_f = at("coef_f", 6 * KB, [128, D], F32)
        coef_m = at("coef_m", 10 * KB, [128, D], F32)
        relu_sc = [at(f"relu{i}", 10 * KB + i * 2 * KB, [128, 512], F32) for i in range(2)]
        xs = [at(f"xs{i}", 14 * KB + i * 4 * KB, [128, D], F32) for i in range(2)]
        xn = [at(f"xn{i}", 22 * KB + i * 2 * KB, [128, D], BF16) for i in range(2)]
        tmp = at("tmp", 26 * KB, [128, D], F32)
        TB = 30 * KB
        cos_t = at("cos_t", TB, [128, NT, 64], F32)
        sin_t = at("sin_t", TB + 4 * KB, [128, NT, 64], F32)
        qrot = [at(f"qrot{i}", TB + 8 * KB + i * KB, [128, 512], BF16) for i in range(4)]
        sTm = [at(f"sTm{i}", TB + 12 * KB + i * KB, [128, 512], BF16) for i in range(2)]
        Sbf = [at(f"Sbf{i}", TB + 14 * KB + i * KB, [128, 512], BF16) for i in range(2)]
        Rst = at("Rst", 14 * KB, [128, 512], F32)
        onb = [at(f"onb{i}", 16 * KB + i * KB, [128, 512], BF16) for i in range(2)]
        wo_slab = [at(f"wo{i}", TB + i * 8 * KB, [128, 8, 512], BF16) for i in range(2)]
        h2T = [at(f"h2T{i}", TB + i * 8 * KB, [128, 8, 512], BF16) for i in range(2)]
        MB = 46 * KB
        xn_all = at("xn_all", MB, [128, NT, D], BF16)
        pooledT = at("pooledT", MB + 16 * KB, [128, 4, S_TOK], BF16)
        HT_ = NT // 2
        qT = at("qT", MB, [128, 4, HT_ * 128], BF16)
        kT = at("kT", MB + 8 * KB, [128, 4, HT_ * 128], BF16)
        k_tok = at("k_tok", MB + 16 * KB, [128, HT_, 512], BF16)
        v_tok = at("v_tok", MB + 24 * KB, [128, HT_, 512], BF16)
        mergedT = at("mergedT", MB, [128, NB, 8, 512], BF16)
        W1B = 78 * KB
        hT = at("hT", W1B, [128, NB, 8, 512], BF16)
        sgT = at("sgT", W1B + 32 * KB, [128, 4, S_TOK], BF16)
        gate_sg = [at(f"gate_sg{i}", W1B + 32 * KB + i * 8 * KB, [128, 8, 512], BF16) for i in range(2)]
        cf = at("cf", W1B + 48 * KB, [128, CF_N], F32)
        assert CF_N * 4 <= 7 * KB
        poolw = at("poolw", W1B + 55 * KB, [128, 4, 128], BF16)
        esc = [at(f"esc{i}", W1B + 48 * KB + i * 2 * KB, [128, 512], F32) for i in range(8)]
        w1 = [at(f"w1_{i}", W1B + i * 8 * KB, [128, 8, 512], BF16) for i in range(8)]
        W2B = 142 * KB
        mixedT = at("mixedT", W2B, [128, NB, 4, 512], BF16)
        u_tok = at("u_tok", W2B, [128, NT, 512], BF16)
        retoutT = at("retoutT", W2B + 16 * KB, [128, NB, 4, 512], BF16)
        ring = [at(f"ring{i}", W2B + 32 * KB + i * 8 * KB, [128, 8, 512], BF16) for i in range(4)]
        wbp_s = at("wbp_s", W2B + 48 * KB, [128, 8, 512], BF16)
        wbr_s = at("wbr_s", W2B + 56 * KB, [128, 8, 512], BF16)
        w2 = [at(f"w2_{i}", (W2B + i * 8 * KB) if i < 4 else (MB + (i - 4) * 8 * KB), [128, 4, D], BF16) for i in range(8)]
        hidden = at("hidden", W2B + 32 * KB, [128, 32, 512], BF16)
        junk = at("junk", 206 * KB, [128, D], BF16)

        ps = es.enter_context(nc.psum_tensor("ps", [128, 8, 512], F32))
        sems = {e: es.enter_context(nc.semaphore("s_" + e)) for e in Sched.ENGS}

        def mkkey(name, wait_total=False):
            k = S.key(name, wait_total)
            k.sem = es.enter_context(nc.semaphore("k_" + name))
            return k

        k_const = mkkey("const", True)
        k_xs = [mkkey(f"xs{i}") for i in range(2)]
        k_ring = [mkkey(f"ring{i}") for i in range(4)]
        k_wo = [mkkey(f"wo{i}") for i in range(2)]
        k_w1 = [mkkey(f"w1_{i}") for i in range(8)]
        k_w2 = [mkkey(f"w2_{i}") for i in range(8)]
        k_misc = mkkey("misc")
        k_scr = [mkkey("scr0"), mkkey("scr1")]
        k_wb = [mkkey("wbp"), mkkey("wbr")]
        dbg_keys = []

        Bc = Buf("consts"); Bmod = Buf("mod"); Bcoef = [Buf("coef_m"), Buf("coef_f")]; Bscr = [Buf("scr0"), Buf("scr1")]
        Bams = Buf("am_sm"); Bafs = Buf("af_sf"); Bcrow = Buf("crow"); Bcmc = Buf("cmc")
        Bxs = [Buf(f"xs{i}") for i in range(2)]; Bxn = [Buf(f"xn{i}") for i in range(2)]; Btmp = Buf("tmp")
        Bstat = [Buf(f"stat{i}") for i in range(16)]
        Bmodsb = Buf("modsb")
        Bjunk = Buf("junk")
        Bcf = Buf("cf"); Bwb = [Buf("wbp"), Buf("wbr")]
        Bps = [Buf(f"ps{i}") for i in range(8)]
        Bxna = [Buf(f"xna{t}") for t in range(NT)]
        BhT = [Buf(f"hT{t}") for t in range(NT)]
        Bring = [Buf(f"ring{i}") for i in range(4)]
        Bwo = [Buf(f"wo{i}") for i in range(2)]
        Bw1 = [Buf(f"w1_{i}") for i in range(8)]
        Bw2 = [Buf(f"w2_{i}") for i in range(8)]
        Btab = Buf("tables"); Brsc = Buf("rsc"); Bqrot = [Buf(f"qrot{i}") for i in range(4)]
        Bu = [Buf(f"u{t}") for t in range(NT)]; Bk = [Buf(f"k{t}") for t in range(NT)]; Bv = [Buf(f"v{t}") for t in range(NT)]
        BqT = [Buf(f"qT{t}") for t in range(NT)]; BkT = [Buf(f"kT{t}") for t in range(NT)]
        Bsg = [[Buf(f"sg{h}_{b}") for b in range(NB)] for h in range(4)]
        Bpl = [[Buf(f"pl{g}_{b}") for b in range(NB)] for g in range(4)]
        Bmx = [[Buf(f"mx{g}_{b}") for b in range(NB)] for g in range(4)]
        Bro = [Buf(f"ro{t}") for t in range(NT)]
        BsTm = [Buf("sTm0"), Buf("sTm1")]; BR = Buf("R"); BSbf = [Buf("Sbf0"), Buf("Sbf1")]; Bon = [Buf("on0"), Buf("on1")]
        Bgn = [Buf("gn0"), Buf("gn1")]
        Bgst = [Buf("gst0"), Buf("gst1")]; Bgmv = [Buf("gmv0"), Buf("gmv1")]; Bgsq = [Buf("gsq0"), Buf("gsq1")]
        Bgrs = [Buf("grs0"), Buf("grs1")]; Bgnb = [Buf("gnb0"), Buf("gnb1")]
        Besc = [Buf(f"esc{i}") for i in range(8)]
        Bmg = [[Buf(f"mg{c}_{b}") for b in range(NB)] for c in range(8)]
        Bx1d = [Buf(f"x1d{t}") for t in range(NT)]
        Bh2 = [Buf("h2T0"), Buf("h2T1")]
        Bhid = [Buf(f"hid{f}") for f in range(32)]
        Brelu = [Buf("relu0"), Buf("relu1")]
        Bpoolw = Buf("poolw")

        def dma(eng, out, in_, reads, writes, key):
            S.op(eng, lambda e: e.dma_start(out=out, in_=in_), reads, writes, dma_key=key)

        def act(out, in_, func, reads, writes, **kw):
            S.op("act", lambda e: e.activation(out=out, in_=in_, func=func, **kw), reads, writes)

        def tt(out, in0, in1, op, reads, writes):
            S.op("dve", lambda e: e.tensor_tensor(out=out, in0=in0, in1=in1, op=op), reads, writes)

        def ts(out, in0, s1, s2, op0, op1, reads, writes):
            if op1 is None:
                S.op("dve", lambda e: e.tensor_scalar(out=out, in0=in0, scalar1=s1, scalar2=None, op0=op0), reads, writes)
            else:
                S.op("dve", lambda e: e.tensor_scalar(out=out, in0=in0, scalar1=s1, scalar2=s2, op0=op0, op1=op1), reads, writes)

        def stt(out, in0, scalar, in1, op0, op1, reads, writes):
            S.op("dve", lambda e: e.scalar_tensor_tensor(out=out, in0=in0, scalar=scalar, in1=in1, op0=op0, op1=op1), reads, writes)

        def cp(out, in_, reads, writes):
            S.op("dve", lambda e: e.tensor_copy(out=out, in_=in_), reads, writes)

        def mms(lst, reads, writes):
            def fn(e):
                ins = None
                for (o_, l_, r_, st_, sp_) in lst:
                    ins = e.matmul(o_, lhsT=l_, rhs=r_, start=st_, stop=sp_)
                return ins
            S.op("pe", fn, reads, writes)

        def trs(lst, reads, writes, ident):
            def fn(e):
                ins = None
                for (o_, i_) in lst:
                    ins = e.transpose(out=o_, in_=i_, identity=ident)
                return ins
            S.op("pe", fn, reads, writes)

        def evac_scaled(dst_fn, pv, a_t, s_t, rd, wr, n_act=3):
            for c in range(8):
                src = pv[:, c * 128:(c + 1) * 128]
                if c < n_act:
                    act(dst_fn(c), src, AF.Identity, rd, wr, scale=a_t[:, c:c + 1], bias=s_t[:, c:c + 1])
                else:
                    ts(dst_fn(c), src, a_t[:, c:c + 1], s_t[:, c:c + 1], ALU.mult, ALU.add, rd, wr)

        ident_b = cb[:, CB_IDENT:CB_IDENT + 128]
        ident_f = cf[:, CF_IDENT:CF_IDENT + 128]

        def psb(i):
            return ps[:, i, :].bitcast(BF16)

        def ps2(i):
            return ps[:, i:i + 2, :].rearrange("p a b -> p (a b)")

        def slab_src(w_ap, c0, ncols=512):
            return w_ap.rearrange("(c p) n -> p c n", p=128)[:, :, c0:c0 + ncols]

        ring_i = [0]

        def load_slab(src_ap, slot=None):
            if slot is None:
                i = ring_i[0] % 4
                ring_i[0] += 1
            else:
                i = slot
            dma("pool", ring[i][:], src_ap, [], [Bring[i]], k_ring[i])
            return i

        def rmsnorm_front(src, srcB, t_col, xn_out, xn_B, on_pool=False, extra_w=()):
            c0 = t_col * 3
            Bst = Bstat[t_col]
            act(xn_out, src, AF.Square, [srcB], [xn_B, Bst] + list(extra_w), accum_out=stat[:, c0:c0 + 1])
            act(stat[:, c0 + 1:c0 + 2], stat[:, c0:c0 + 1], AF.Sqrt, [Bst, Bc], [Bst], scale=1.0 / D, bias=epsb[:])
            S.op("dve", lambda e: e.reciprocal(out=stat[:, c0 + 2:c0 + 3], in_=stat[:, c0 + 1:c0 + 2]), [Bst], [Bst])
            if on_pool:
                S.op("pool", lambda e: e.tensor_scalar(out=xn_out, in0=src, scalar1=stat[:, c0 + 2:c0 + 3], scalar2=1.0, op0=ALU.mult, op1=ALU.mult),
                     [srcB, Bst], [xn_B])
            else:
                ts(xn_out, src, stat[:, c0 + 2:c0 + 3], None, ALU.mult, None, [srcB, Bst], [xn_B])

        def TAP(name, src_ap, shape, dt, rd):
            if not debug:
                return
            d_ = nc.dram_tensor("dbg_" + name, shape, dt, kind="ExternalOutput").ap()
            kd = mkkey("dbg_" + name)
            dbg_keys.append(kd)
            dma("sp", d_, src_ap, rd, [], kd)

        dma("sp", cc[:], cc_d, [], [Bc], k_const)
        dma("sp", cf[:], cf_d, [], [Bcf], k_const)
        dma("sp", cb[:], cb_d, [], [Bc], k_const)
        Bcb2 = Buf("cb2")
        dma("sp", cb2[:], cb2_d, [], [Bcb2], k_const)
        dma("sp", pos_i[:], pos_d, [], [Bc], k_const)
        S.op("dve", lambda e: e.memset(epsb[:], EPS), [], [Bc])
        S.op("dve", lambda e: e.memset(onec[:], 1.0), [], [Bc])
        S.op("dve", lambda e: e.memset(zeroc[:], 0.0), [], [Bc])
        for j in range(2):
            act(scb[:, :, j], cc[:, CC_C:CC_C + 8], AF.Silu, [Bc], [Bmod])

        cp(pos_f[:], pos_i[:], [Bc], [Btab])
        MAGIC = 12582912.0
        invf_b = cf[:, CF_INVF:CF_INVF + 64]
        ang = cos_t
        for (dst, shift) in ((sin_t, 0.0), (cos_t, PI / 2)):
            a_ = dst[:]
            tt(a_, invf_b.unsqueeze(1).broadcast_to([128, NT, 64]), pos_f[:].unsqueeze(2).broadcast_to([128, NT, 64]),
               ALU.mult, [Bcf, Btab], [Btab])
            sc1 = xs[0][:, 0:NT * 64].rearrange("p (a b) -> p a b", b=64)
            ts(sc1, a_, shift, 1.0 / (2 * PI), ALU.add, ALU.mult, [Btab], [Bxs[0]])
            ts(sc1, sc1, MAGIC, None, ALU.add, None, [Bxs[0]], [Bxs[0]])
            ts(sc1, sc1, MAGIC, None, ALU.subtract, None, [Bxs[0]], [Bxs[0]])
            stt(a_, sc1, -6.28125, a_, ALU.mult, ALU.add, [Btab, Bxs[0]], [Btab])
            stt(a_, sc1, -0.0019352436065673828, a_, ALU.mult, ALU.add, [Btab, Bxs[0]], [Btab])
            stt(a_, sc1, -6.357301884918343e-08, a_, ALU.mult, ALU.add, [Btab, Bxs[0]], [Btab])
            if shift != 0.0:
                ts(a_, a_, shift, None, ALU.add, None, [Btab], [Btab])
            ts(a_, a_, PI, -PI, ALU.min, ALU.max, [Btab], [Btab])
            act(a_, a_, AF.Sin, [Btab], [Btab])

        ADA_ORDER = [2, 3, 0, 1, 4, 5, 6, 7, 8, 9, 10, 11]

        def ada_slab(sl, i=None, bank=5, evac=True):
            if i is None:
                i = load_slab(slab_src(adaw_d, sl * 512))
            lst = []
            for j in range(4):
                col = sl * 4 + j
                for k in range(8):
                    lst.append((ps[:, bank, 2 * col:2 * col + 2], ring[i][:, k, j * 128:(j + 1) * 128], scb[:, k, :], k == 0, k == 7))
            mms(lst, [Bring[i], Bmod], [Bps[bank]])
            if evac:
                ada_evac(sl, bank)

        def ada_evac(sl, bank=5):
            cp(modsb[:, sl * 4:sl * 4 + 4], ps[:, bank, 8 * sl:8 * sl + 8].rearrange("p (k two) -> p k two", two=2)[:, :, 0],
               [Bps[bank]], [Bmodsb])

        for sl in ADA_ORDER[:4]:
            ada_slab(sl, evac=False)
        k_tmp = mkkey("tmp")
        slots = [(xs[0], Bxs[0], k_xs[0]), (xs[1], Bxs[1], k_xs[1]), (tmp, Btmp, k_tmp)]
        def proj_tok(slab_i, t, bank, n=512):
            mms([(ps[:, bank, 0:n], hT[:, t // 4, k, (t % 4) * 128:(t % 4 + 1) * 128], ring[slab_i][:, k, 0:n], k == 0, k == 7) for k in range(8)],
                [BhT[t], Bring[slab_i]], [Bps[bank]])


        def a_front(t):
            x_t, x_B, x_k = slots[t % 3]
            dma("sp", x_t[:], x_d[t * 128:(t + 1) * 128, :], [Bring[3]], [x_B], x_k)
            rmsnorm_front(x_t[:], x_B, t, xn_all[:, t, :], Bxna[t], on_pool=True)

        def a_back(t):
            bk = 6 + (t % 2)
            pv = psb(bk)
            trs([(pv[:, c * 128:(c + 1) * 128], xn_all[:, t, c * 128:(c + 1) * 128]) for c in range(8)],
                [Bxna[t], Bc], [Bps[bk]], ident_b)
            evac_scaled(lambda c, t=t: hT[:, t // 4, c, (t % 4) * 128:(t % 4 + 1) * 128], pv, am, sm, [Bps[bk], Bams], [BhT[t]], n_act=2)

        def modcol(v):
            return modsb[:, 8 * v:8 * v + 8]

        a_front(0)
        a_front(1)
        for sl in ADA_ORDER[:4]:
            ada_evac(sl)
        stt(am[:], modcol(1), 1.0, cc[:, CC_ADAB + 8:CC_ADAB + 16], ALU.add, ALU.add, [Bmodsb, Bc], [Bams])
        tt(am[:], am[:], cc[:, CC_GPM:CC_GPM + 8], ALU.mult, [Bams, Bc], [Bams])
        tt(sm[:], modcol(0), cc[:, CC_ADAB:CC_ADAB + 8], ALU.add, [Bmodsb, Bc], [Bams])
        si_u = load_slab(slab_src(win_d, 0))

        def proj_u(t):
            bank = t % 4
            proj_tok(si_u, t, bank)
            cp(u_tok[:, t, :], ps[:, bank, :], [Bps[bank]], [Bu[t]])

        for t in range(2, NT):
            a_front(t)
            a_back(t - 2)
            if t >= 5:
                proj_u(t - 5)
        a_back(NT - 2)
        a_back(NT - 1)
        for t in range(NT - 5, NT):
            proj_u(t)

        TAP("hT", hT[:], [128, NB, 8, 512], BF16, BhT)
        TAP("am", am[:], [128, 8], F32, [Bams]); TAP("sm", sm[:], [128, 8], F32, [Bams])

        Mguard0 = Bxna
        for sl in (4, 5):
            ada_slab(sl)
        TAP("u_tok", u_tok[:], [128, NT, 512], BF16, Bu)

        dma("pool", poolw[:], poolw_d, [], [Bpoolw], k_misc)
        for gi in range(4):
            base = CB_POOLA + gi * 384
            for b in range(NB):
                bank = (gi * NB + b) % 4
                lst = []
                for tl in range(4):
                    t = b * 4 + tl
                    o_ = ps[:, bank, tl * 128:(tl + 1) * 128]
                    if t == 0:
                        lst.append((o_, u_tok[:, 0, gi * 128:(gi + 1) * 128], cb[:, base:base + 128], True, False))
                        lst.append((o_, u_tok[:, 0, gi * 128:(gi + 1) * 128], cb2[:, gi * 128:(gi + 1) * 128], False, True))
                    else:
                        lst.append((o_, u_tok[:, t, gi * 128:(gi + 1) * 128], cb[:, base + 128:base + 256], True, False))
                        lst.append((o_, u_tok[:, t - 1, gi * 128:(gi + 1) * 128], cb[:, base + 256:base + 384], False, True))
                rd = [Bu[t] for t in range(max(0, b * 4 - 1), b * 4 + 4)] + [Bc, Bcb2]
                mms(lst, rd, [Bps[bank]])
                act(pooledT[:, gi, b * 512:(b + 1) * 512], ps[:, bank, :], AF.Copy, [Bps[bank]], [Bpl[gi][b]] + (Mguard0 if (gi == 0 and b == 0) else []))
        for gi in range(4):
            for b in range(NB):
                bank = (gi * NB + b) % 4
                mms([(ps[:, bank, :], poolw[:, gi, :], pooledT[:, gi, b * 512:(b + 1) * 512], True, True)],
                    [Bpoolw, Bpl[gi][b]], [Bps[bank]])
                ts(mixedT[:, b, gi, :], ps[:, bank, :], cc[:, CC_PS + gi:CC_PS + gi + 1], None, ALU.mult, None,
                   [Bps[bank], Bc], [Bmx[gi][b]] + (Bu if (gi == 0 and b == 0) else []))

        si = load_slab(slab_src(win_d, 2048))
        for h in range(4):
            for b in range(NB):
                bank = (h * NB + b) % 4
                mms([(ps[:, bank, :], ring[si][:, k, h * 128:(h + 1) * 128], hT[:, b, k, :], k == 0, k == 7) for k in range(8)],
                    [Bring[si]] + BhT[b * 4:b * 4 + 4], [Bps[bank]])
                act(sgT[:, h, b * 512:(b + 1) * 512], ps[:, bank, :], AF.Silu, [Bps[bank]], [Bsg[h][b]])
        for sl in (6, 7):
            ada_slab(sl)

        Mguard1 = Bu + [x for r in Bpl for x in r]

        def finalize_mod():
            stt(af[:], modcol(4), 1.0, cc[:, CC_ADAB + 32:CC_ADAB + 40], ALU.add, ALU.add, [Bmodsb, Bc], [Bafs])
            tt(af[:], af[:], cc[:, CC_GPF:CC_GPF + 8], ALU.mult, [Bafs, Bc], [Bafs])
            tt(sf[:], modcol(3), cc[:, CC_ADAB + 24:CC_ADAB + 32], ALU.add, [Bmodsb, Bc], [Bafs])
            tt(cmc[:], modcol(2), cc[:, CC_ADAB + 16:CC_ADAB + 24], ALU.add, [Bmodsb, Bc], [Bcmc])
            tt(cmc[:], cmc[:], cc[:, CC_GQM:CC_GQM + 8], ALU.mult, [Bcmc, Bc], [Bcmc])
            tt(cfc[:], modcol(5), cc[:, CC_ADAB + 40:CC_ADAB + 48], ALU.add, [Bmodsb, Bc], [Bcmc])
            tt(cfc[:], cfc[:], cc[:, CC_GQF:CC_GQF + 8], ALU.mult, [Bcmc, Bc], [Bcmc])
            trs([(ps[0:8, 4, 0:128], cmc[:]), (ps[0:8, 4, 128:256], cfc[:])], [Bcmc, Bcf], [Bps[4]], ident_f)
            cp(crow[:], ps[0:8, 4, 0:256], [Bps[4]], [Bcrow, Bcb2])
            for v in range(2):
                dma("sp", scr_d[v:v + 1, :].rearrange("o (k p) -> (o k) p", p=128), crow[:, v * 128:(v + 1) * 128], [Bcrow], [Bscr[v]], k_scr[v])
            for v, dst in enumerate((coef_m, coef_f)):
                dma("sp", dst[:], scr_d[v:v + 1, :].broadcast_to([128, D]), [Bscr[v]], [Bcoef[v]], k_scr[v])


        TAP("mixedT", mixedT[:], [128, NB, 4, 512], BF16, [x for r in Bmx for x in r])
        TAP("sgT", sgT[:], [128, 4, S_TOK], BF16, [x for r in Bsg for x in r])
        TAP("cos", cos_t[:], [128, NT, 64], F32, [Btab]); TAP("sin", sin_t[:], [128, NT, 64], F32, [Btab])

        rsets = [(tmp, Btmp), (xs[1], Bxs[1])]
        i_q = load_slab(slab_src(win_d, 512))
        i_k = load_slab(slab_src(win_d, 1024))
        i_v = load_slab(slab_src(win_d, 1536))
        i_a8 = load_slab(slab_src(adaw_d, 8 * 512))
        dq4 = cf[:, CF_DQ:CF_DQ + 512].rearrange("p (h i) -> p h i", h=4)
        dvb4 = cf[:, CF_DV:CF_DV + 4].unsqueeze(2).broadcast_to([128, 4, 128])
        maskT = cf[:, CF_MASK:CF_MASK + 512]
        cdt = cf[:, CF_CD:CF_CD + 512]
        OB = [2, 3, 5]

        def qk_finish(T0, tt_, first):
            qi = tt_ % 4
            pq, pk = psb(6), psb(7)
            trs([(pq[:, hh * 128:(hh + 1) * 128], qrot[qi][:, hh * 128:(hh + 1) * 128]) for hh in range(4)], [Bqrot[qi], Bc], [Bps[6]], ident_b)
            trs([(pk[:, hh * 128:(hh + 1) * 128], k_tok[:, tt_, hh * 128:(hh + 1) * 128]) for hh in range(4)], [Bk[tt_], Bc], [Bps[7]], ident_b)
            tsl = slice(tt_ * 128, (tt_ + 1) * 128)
            tt(qT[:, :, tsl], pq[:, 0:512].rearrange("p (h i) -> p h i", h=4), dq4, ALU.mult, [Bps[6], Bcf], [BqT[tt_]] + first)
            for hh in range(4):
                act(kT[:, hh, tsl], pk[:, hh * 128:(hh + 1) * 128], AF.Identity, [Bps[7], Bc], [BkT[tt_]] + first,
                    scale=onec[:], bias=zeroc[:])

        def qk_pass(T0, guard):
            pend = []
            for tt_ in range(HT_):
                t = T0 + tt_
                qi = tt_ % 4
                for is_k in range(2):
                    bank = (2 * tt_ + is_k) % 4
                    proj_tok(i_k if is_k else i_q, t, bank, 512)
                    rs_t, rs_B = rsets[is_k]
                    r_ = [rs_t[:, i * 256:(i + 1) * 256].rearrange("p (g j) -> p g j", g=4) for i in range(4)]
                    pvw = ps[:, bank, :].rearrange("p (g two j) -> p g two j", g=4, two=2)
                    x1 = pvw[:, :, 0, :]; x2 = pvw[:, :, 1, :]
                    cb_ = cos_t[:, t, :].unsqueeze(1).broadcast_to([128, 4, 64])
                    sb_ = sin_t[:, t, :].unsqueeze(1).broadcast_to([128, 4, 64])
                    tt(r_[0], x1, cb_, ALU.mult, [Bps[bank], Btab], [rs_B])
                    tt(r_[1], x2, sb_, ALU.mult, [Bps[bank], Btab], [rs_B])
                    tt(r_[2], x2, cb_, ALU.mult, [Bps[bank], Btab], [rs_B])
                    tt(r_[3], x1, sb_, ALU.mult, [Bps[bank], Btab], [rs_B])
                    gd = guard if tt_ < 2 else []
                    if is_k:
                        dst = k_tok[:, tt_, :].rearrange("p (h two j) -> p h two j", h=4, two=2)
                        wr = [Bk[tt_]] + gd
                    else:
                        dst = qrot[qi][:].rearrange("p (h two j) -> p h two j", h=4, two=2)
                        wr = [Bqrot[qi]]
                    S.op("pool", (lambda e, dst=dst, r_=r_: e.tensor_tensor(out=dst[:, :, 0, :], in0=r_[0], in1=r_[1], op=ALU.subtract)), [rs_B], wr)
                    S.op("pool", (lambda e, dst=dst, r_=r_: e.tensor_tensor(out=dst[:, :, 1, :], in0=r_[2], in1=r_[3], op=ALU.add)), [rs_B], wr)
                pend.append(tt_)
                if len(pend) > 2:
                    tp = pend.pop(0)
                    qk_finish(T0, tp, guard if tp < 2 else [])
            while pend:
                tp = pend.pop(0)
                qk_finish(T0, tp, guard if tp < 2 else [])

        def v_pass(T0, guard):
            for tt_ in range(HT_):
                bank = tt_ % 4
                proj_tok(i_v, T0 + tt_, bank, 512)
                tt(v_tok[:, tt_, :].rearrange("p (h e) -> p h e", h=4), ps[:, bank, :].rearrange("p (h e) -> p h e", h=4), dvb4,
                   ALU.mult, [Bps[bank], Bcf], [Bv[tt_]] + (guard if tt_ < 2 else []))

        def retention_pass(T0, hooks=None):
            RG = [Bxs[0]]
            ok = lambda T: T0 <= T < T0 + HT_

            def a1(T):
                tt_ = T - T0; p2 = T % 2
                tsl = slice(tt_ * 128, (tt_ + 1) * 128)
                mms([(ps[:, p2, hh * 128:(hh + 1) * 128], kT[:, hh, tsl], qT[:, hh, tsl], True, True) for hh in range(4)],
                    [BkT[tt_], BqT[tt_]], [Bps[p2]])
                tt(sTm[p2][:], ps[:, p2, :], maskT, ALU.mult, [Bps[p2], Bcf], [BsTm[p2]])

            def a2_pe(T):
                tt_ = T - T0; p2 = T % 2
                tsl = slice(tt_ * 128, (tt_ + 1) * 128)
                b_o = OB[T % 3]
                lst = []
                for hh in range(4):
                    hs = slice(hh * 128, (hh + 1) * 128)
                    lst.append((ps[:, b_o, hs], sTm[p2][:, hs], v_tok[:, tt_, hs], True, T == 0))
                    if T > 0:
                        lst.append((ps[:, b_o, hs], qT[:, hh, tsl], Sbf[(T - 1) % 2][:, hs], False, True))
                mms(lst, [BsTm[p2], Bv[tt_], BqT[tt_]] + ([BSbf[(T - 1) % 2]] if T > 0 else []), [Bps[b_o]])
                if T < NT - 1:
                    mms([(ps[:, 4, hh * 128:(hh + 1) * 128], k_tok[:, tt_, hh * 128:(hh + 1) * 128], v_tok[:, tt_, hh * 128:(hh + 1) * 128], True, True)
                         for hh in range(4)], [Bk[tt_], Bv[tt_]], [Bps[4]])

            def st1(T):
                if 0 < T < NT - 1:
                    tt(Rst[:], Rst[:], cdt, ALU.mult, [BR, Bcf], [BR])

            def st2(T):
                if T >= NT - 1:
                    return
                p2 = T % 2
                if T == 0:
                    cp(Rst[:], ps[:, 4, :], [Bps[4]], [BR] + RG)
                else:
                    tt(Rst[:], Rst[:], ps[:, 4, :], ALU.add, [BR, Bps[4]], [BR])
                for hh in range(4):
                    act(Sbf[p2][:, hh * 128:(hh + 1) * 128], Rst[:, hh * 128:(hh + 1) * 128], AF.Copy, [BR], [BSbf[p2]],
                        scale=float((1.0 - 2.0 ** (-5.0 - hh)) ** 128))

            def b_stats(T):
                p2 = T % 2; b_o = OB[T % 3]
                for hh in range(4):
                    S.op("dve", (lambda e, hh=hh: e.bn_stats(out=gst[:, p2, hh, :], in_=ps[:, b_o, hh * 128:(hh + 1) * 128])),
                         [Bps[b_o]], [Bgst[p2]])

            def b_aggr(T):
                p2 = T % 2
                for hh in range(4):
                    S.op("dve", (lambda e, hh=hh: e.bn_aggr(out=gmv[:, p2, hh, :], in_=gst[:, p2, hh, :])), [Bgst[p2]], [Bgmv[p2]])
                act(gsq[:, p2, :], gmv[:, p2, :, 1], AF.Sqrt, [Bgmv[p2], Bc], [Bgsq[p2]], bias=epsb[:])

            def b_recip(T):
                p2 = T % 2
                S.op("dve", (lambda e: e.reciprocal(out=grs[:, p2, :], in_=gsq[:, p2, :])), [Bgsq[p2]], [Bgrs[p2]])

            def b_norm(T):
                p2 = T % 2; b_o = OB[T % 3]
                stt(gnb[:, p2, :], gmv[:, p2, :, 0], -1.0, grs[:, p2, :], ALU.mult, ALU.mult, [Bgmv[p2], Bgrs[p2]], [Bgnb[p2]])
                for hh in range(4):
                    act(onb[p2][:, hh * 128:(hh + 1) * 128], ps[:, b_o, hh * 128:(hh + 1) * 128], AF.Identity,
                        [Bps[b_o], Bgrs[p2], Bgnb[p2]], [Bon[p2]] + (RG if T < 2 else []), scale=grs[:, p2, hh:hh + 1], bias=gnb[:, p2, hh:hh + 1])

            def c_tr(T):
                p2 = T % 2; b_t = 6 + p2
                pv = psb(b_t)
                trs([(pv[:, hh * 128:(hh + 1) * 128], onb[p2][:, hh * 128:(hh + 1) * 128]) for hh in range(4)], [Bon[p2], Bc], [Bps[b_t]], ident_b)

            def c_out(T):
                p2 = T % 2; b_t = 6 + p2
                pv = psb(b_t)
                tt(retoutT[:, T // 4, :, (T % 4) * 128:(T % 4 + 1) * 128], pv[:, 0:512].rearrange("p (h i) -> p h i", h=4),
                   sgT[:, :, T * 128:(T + 1) * 128], ALU.mult,
                   [Bps[b_t]] + [Bsg[hh][T // 4] for hh in range(4)], [Bro[T]])

            a1(T0)
            for i in range(T0, T0 + HT_ + 3):
                if ok(i + 1): a1(i + 1)
                if ok(i - 1): b_stats(i - 1)
                if ok(i):
                    a2_pe(i)
                    st1(i)
                if ok(i - 3): c_tr(i - 3)
                if ok(i - 2): b_recip(i - 2)
                if ok(i):
                    st2(i)
                    if hooks and (i - T0) in hooks:
                        hooks[i - T0]()
                if ok(i - 1): b_aggr(i - 1)
                if ok(i - 2): b_norm(i - 2)
                if ok(i - 3): c_out(i - 3)

        def taps_qkv(ps_):
            TAP(f"qT{ps_}", qT[:], [128, 4, HT_ * 128], BF16, BqT[:HT_]); TAP(f"kT{ps_}", kT[:], [128, 4, HT_ * 128], BF16, BkT[:HT_])
            TAP(f"k_tok{ps_}", k_tok[:], [128, HT_, 512], BF16, Bk[:HT_]); TAP(f"v_tok{ps_}", v_tok[:], [128, HT_, 512], BF16, Bv[:HT_])

        def ada_step(sl):
            def fn():
                ada_slab(sl, i_a8, bank=4)
                if sl < 11:
                    load_slab(slab_src(adaw_d, (sl + 1) * 512), slot=i_a8)
            return fn
        qk_pass(0, Mguard1)
        v_pass(0, Mguard1)
        taps_qkv(0)
        retention_pass(0, {0: ada_step(8), 2: ada_step(9), 4: ada_step(10), 6: ada_step(11)})
        qk_pass(HT_, [])
        v_pass(HT_, [])
        taps_qkv(1)
        SGall = [x for r in Bsg for x in r]
        gsl = [[ring[0], gate_sg[0]], [ring[1], gate_sg[1]]]
        Bgs = [[Bring[0], Buf("gate_sg0")], [Bring[1], Buf("gate_sg1")]]
        k_gs = [[k_ring[0], mkkey("gsg0")], [k_ring[1], mkkey("gsg1")]]
        for g in range(2):
            dma("pool", gsl[g][0][:], slab_src(win_d, 2560 + g * 1024), [], [Bgs[g][0]], k_gs[g][0])
        for wi, (w_s, w_dd) in enumerate(((wbp_s, wbp_d), (wbr_s, wbr_d))):
            dma("pool", w_s[:].rearrange("p (c a) n -> p c (a n)", a=2), w_dd.rearrange("(c p) n -> p c n", p=128),
                [], [Bwb[wi], Bring[2 + wi]], k_wb[wi])
        retention_pass(HT_)
        finalize_mod()
        TAP("coef_m", coef_m[:], [128, D], F32, [Bcoef[0]]); TAP("coef_f", coef_f[:], [128, D], F32, [Bcoef[1]])
        TAP("af", af[:], [128, 8], F32, [Bafs]); TAP("sf", sf[:], [128, 8], F32, [Bafs])
        TAP("retoutT", retoutT[:], [128, NB, 4, 512], BF16, Bro)

        for g in range(2):
            dma("pool", gsl[g][1][:], slab_src(win_d, 2560 + g * 1024 + 512), [], [Bgs[g][1]] + SGall, k_gs[g][1])
        WOguard = BsTm + BSbf + Bon + [BR, Btab] + Bqrot
        for i in range(2):
            dma("pool", wo_slab[i][:], slab_src(wout_d, i * 512), [], [Bwo[i]] + WOguard, k_wo[i])

        def post_norm_residual(t, ybank, coef, coefB, src_ap, srcB_list, dst_ap, dstB_list, col):
            sl = t % 2
            y = ps2(ybank)
            c0 = col * 3
            Bst = Bstat[col]
            act(junk[:], y, AF.Square, [Bps[ybank], Bps[ybank + 1]], [Bjunk, Bst], accum_out=stat[:, c0:c0 + 1])
            act(stat[:, c0 + 1:c0 + 2], stat[:, c0:c0 + 1], AF.Sqrt, [Bst, Bc], [Bst], scale=1.0 / D, bias=epsb[:])
            S.op("dve", lambda e: e.reciprocal(out=stat[:, c0 + 2:c0 + 3], in_=stat[:, c0 + 1:c0 + 2]), [Bst], [Bst])
            dma("sp", xs[sl][:], src_ap, srcB_list, [Bxs[sl]] + ([BR] + Bon if (t == 0 and coef is coef_m) else []), k_xs[sl])
            stt(tmp[:], y, stat[:, c0 + 2:c0 + 3], coef[:], ALU.mult, ALU.mult, [Bps[ybank], Bps[ybank + 1], Bst, coefB], [Btmp])
            tt(xs[sl][:], xs[sl][:], tmp[:], ALU.add, [Bxs[sl], Btmp], [Bxs[sl]])
            dma("sp", dst_ap, xs[sl][:], [Bxs[sl]], dstB_list, k_xs[sl])

        Eguard = [Bcf, Bpoolw]
        MguardE = BqT + BkT + Bk + Bv
        unit = 0
        h2T_early = at("h2T_early", MB + 16 * KB, [128, 8, 512], BF16)
        Bh2e = Buf("h2T_early")
        stE = at("stE", MB + 8 * KB, [128, D], F32)
        xnE = [at(f"xnE{i}", MB + 12 * KB + i * 2 * KB, [128, D], BF16) for i in range(2)]
        Eg1 = [Bmg[c][1] for c in range(8)]
        BstE = Buf("stE"); BxnE = [Buf("xnE0"), Buf("xnE1")]
        k_stE = mkkey("stE")
        XN0 = [(xn[0], Bxn[0]), (xn[1], Bxn[1]), (xnE[0], BxnE[0]), (xnE[1], BxnE[1])]

        def early_prep_back(tl, bk):
            pv = psb(bk)
            trs([(pv[:, c * 128:(c + 1) * 128], XN0[tl][0][:, c * 128:(c + 1) * 128]) for c in range(8)], [XN0[tl][1], Bc], [Bps[bk]], ident_b)
            evac_scaled(lambda c: h2T_early[:, c, tl * 128:(tl + 1) * 128], pv, af, sf, [Bps[bk], Bafs],
                        [Bh2e] + ([Bmg[c][2] for c in range(8)] if tl == 0 else []))

        def early_load(tl):
            dma("sp", stE[:], x1_d[tl * 128:(tl + 1) * 128, :], [Bx1d[tl]], [BstE] + (Eg1 if tl == 0 else []), k_stE)

        ucnt = [0]

        def e_units(b, dcs):
            rdh = BhT[b * 4:b * 4 + 4]
            for dc in dcs:
                unit = ucnt[0]
                a_, off = dc // 4, (dc % 4) * 128
                bb = (unit % 2) * 4
                e0 = (unit % 2) * 4
                bks = [0, 1, 2, 3] if bb == 0 else [4, 5, 6, 7]
                for g in range(2):
                    mms([(ps[:, bks[g], :], gsl[g][a_][:, k, off:off + 128], hT[:, b, k, :], k == 0, k == 7) for k in range(8)],
                        [Bgs[g][a_]] + rdh, [Bps[bks[g]]])
                mms([(ps[:, bks[2], :], wbp_s[:, gi * 2 + a_, off:off + 128], mixedT[:, b, gi, :], gi == 0, gi == 3) for gi in range(4)],
                    [Bwb[0]] + [Bmx[gi][b] for gi in range(4)], [Bps[bks[2]]])
                mms([(ps[:, bks[3], :], wbr_s[:, h * 2 + a_, off:off + 128], retoutT[:, b, h, :], h == 0, h == 3) for h in range(4)],
                    [Bwb[1]] + Bro[b * 4:b * 4 + 4], [Bps[bks[3]]])
                gd = Eguard if unit < 2 else []
                for g in range(2):
                    act(esc[e0 + g][:], ps[:, bks[g], :], AF.Sigmoid, [Bps[bks[g]], Bc], [Besc[e0 + g]] + gd,
                        bias=cc[:, CC_BG + g * 8 + dc:CC_BG + g * 8 + dc + 1])
                tt(esc[e0 + 2][:], esc[e0][:], ps[:, bks[2], :], ALU.mult, [Besc[e0], Bps[bks[2]]], [Besc[e0 + 2]] + gd)
                tt(esc[e0 + 3][:], esc[e0 + 1][:], ps[:, bks[3], :], ALU.mult, [Besc[e0 + 1], Bps[bks[3]]], [Besc[e0 + 3]] + gd)
                tt(mergedT[:, b, dc, :], esc[e0 + 2][:], esc[e0 + 3][:], ALU.add, [Besc[e0 + 2], Besc[e0 + 3]],
                   [Bmg[dc][b]] + (MguardE if unit == 0 else []))
                ucnt[0] += 1
                if b == 3 and 2 <= dc <= 5:
                    tl_ = dc - 2
                    rmsnorm_front(stE[:], BstE, 4 + tl_, XN0[tl_][0][:], XN0[tl_][1], extra_w=(Eg1 if tl_ >= 2 else ()))
                    if tl_ < 3:
                        early_load(tl_ + 1)

        def f_tiles(b):
            for tl in range(4):
                t = b * 4 + tl
                yb = tl * 2 if b < 3 else (tl % 2) * 2
                lst = []
                for half in range(2):
                    for c in range(8):
                        lst.append((ps[:, yb + half, :], mergedT[:, b, c, tl * 128:(tl + 1) * 128], wo_slab[half][:, c, :], c == 0, c == 7))
                mms(lst, [Bmg[c][b] for c in range(8)] + Bwo, [Bps[yb], Bps[yb + 1]])
                post_norm_residual(t, yb, coef_m, Bcoef[0], x_d[t * 128:(t + 1) * 128, :], [], x1_d[t * 128:(t + 1) * 128, :], [Bx1d[t]], t % 4)
            TAP(f"mergedT{b}", mergedT[:, b], [128, 8, 512], BF16, [Bmg[c][b] for c in range(8)])

        def w2_load(i_, gl):
            dma("pool", w2[i_][:], wff2_d[i_ * 512:(i_ + 1) * 512, :].rearrange("(c p) n -> p c n", p=128), [], [Bw2[i_]] + gl, k_w2[i_])

        def e_prefetch(b):
            if b < 3:
                dma("pool", w1[b][:], slab_src(wff1_d, b * 512), [], [Bw1[b]] + BhT[b * 4:b * 4 + 4], k_w1[b])
            if b == 0:
                w2_load(4, [Bmg[c][0] for c in range(8)])
            if b == 1:
                w2_load(0, [Bmx[gi][bb_] for gi in range(4) for bb_ in (0, 1)])
                w2_load(2, Bro[0:8])

        e_units(0, range(8))
        for b in range(1, NB):
            if b == 3:
                early_load(0)
            e_units(b, range(0, 2))
            f_tiles(b - 1)
            e_prefetch(b - 1)
            e_units(b, range(2, 8))
        for tl in range(4):
            early_prep_back(tl, 6 + (tl % 2))
        f_tiles(3)
        e_prefetch(3)
        def prep_front(b, tl):
            t = b * 4 + tl
            sl = t % 2
            dma("sp", xs[sl][:], x1_d[t * 128:(t + 1) * 128, :], [Bx1d[t]], [Bxs[sl]], k_xs[sl])
            rmsnorm_front(xs[sl][:], Bxs[sl], 4 + tl, xn[sl][:], Bxn[sl])

        H2guard = Bwo + BsTm + BSbf + Bon + [BR]
        HIDguard = Bring + Bwb
        def prep_back(b, tl, xn_t=None, xn_B=None):
            hb = b % 2
            t = b * 4 + tl
            sl = t % 2
            if xn_t is None:
                xn_t, xn_B = xn[sl], Bxn[sl]
            bk = 6 + (t % 2)
            pv = psb(bk)
            trs([(pv[:, c * 128:(c + 1) * 128], xn_t[:, c * 128:(c + 1) * 128]) for c in range(8)], [xn_B, Bc], [Bps[bk]], ident_b)
            evac_scaled(lambda c: h2T[hb][:, c, tl * 128:(tl + 1) * 128], pv, af, sf, [Bps[bk], Bafs],
                        [Bh2[hb]] + (H2guard if b <= 2 else []))

        def ff1(b):
            hb = b % 2
            h_t, h_B = (h2T_early, Bh2e) if b == 0 else (h2T[hb], Bh2[hb])
            for f in range(32):
                bank = 4 + (f % 2)
                s_, fo = f // 4, (f % 4) * 128
                mms([(ps[:, bank, :], w1[s_][:, k, fo:fo + 128], h_t[:, k, :], k == 0, k == 7) for k in range(8)],
                    [Bw1[s_], h_B], [Bps[bank]])
                r = f % 2
                act(relu_sc[r][:], ps[:, bank, :], AF.Relu, [Bps[bank]], [Brelu[r]] + ([Bcoef[0]] if (b == 0 and f < 2) else []))
                tt(hidden[:, f, :], relu_sc[r][:], relu_sc[r][:], ALU.mult, [Brelu[r]], [Bhid[f]] + (HIDguard if b == 0 else []))

        SLORD = (0, 2, 4, 1, 3, 7, 5, 6)
        forder = [f for sl_ in SLORD for f in range(sl_ * 4, sl_ * 4 + 4)]

        def ff2_tile(b, tl, yb=None, phase="all"):
            t = b * 4 + tl
            if yb is None:
                yb = (t % 2) * 2
            lo, hi = {"all": (0, 32), "A": (0, 12), "B": (12, 32)}[phase]
            lst = []
            for half in range(2):
                for n_ in range(lo, hi):
                    f = forder[n_]
                    lst.append((ps[:, yb + half, :], hidden[:, f, tl * 128:(tl + 1) * 128], w2[f // 4][:, f % 4, half * 512:(half + 1) * 512], n_ == 0, n_ == 31))
            mms(lst, [Bhid[forder[n_]] for n_ in range(lo, hi)] + [Bw2[sl_] for sl_ in SLORD[lo // 4:hi // 4]], [Bps[yb], Bps[yb + 1]])
            if phase != "A":
                post_norm_residual(t, yb, coef_f, Bcoef[1], x1_d[t * 128:(t + 1) * 128, :], [Bx1d[t]], out_d[t * 128:(t + 1) * 128, :], [], 8 + (t % 4))

        chain = [Buf("wchain0"), Buf("wchain1")]
        nch = [0]

        def chained(fn_dma, wbuf_list):
            c = chain[nch[0] % 2]
            nch[0] += 1
            fn_dma(Bxs if nch[0] <= 2 else [], wbuf_list + [c])
        chained(lambda rd, wr: dma("pool", w1[3][:], slab_src(wff1_d, 3 * 512), rd, wr, k_w1[3]), [Bw1[3]] + BhT[12:16])
        for i in (4, 5):
            chained(lambda rd, wr, i=i: dma("pool", w1[i][:], slab_src(wff1_d, i * 512), rd, wr, k_w1[i]), [Bw1[i]] + SGall + [Bgs[0][1], Bgs[1][1]])
        for i in (6, 7):
            chained(lambda rd, wr, i=i: dma("pool", w1[i][:], slab_src(wff1_d, i * 512), [], wr, k_w1[i]), [Bw1[i]] + Besc + [Bcf, Bpoolw])
        for (i_, gl) in ((1, [Bmx[gi][bb_] for gi in range(4) for bb_ in (2, 3)]), (3, Bro[8:16]), (7, [Bmg[c][3] for c in range(8)]),
                         (5, Eg1 + [BstE] + BxnE)):
            chained(lambda rd, wr, i_=i_: dma("pool", w2[i_][:], wff2_d[i_ * 512:(i_ + 1) * 512, :].rearrange("(c p) n -> p c n", p=128), [], wr, k_w2[i_]),
                    [Bw2[i_]] + gl)

        for b in range(NB):
            ff1(b)
            if b == 0:
                TAP("hidden", hidden[:], [128, 32, 512], BF16, Bhid)
                TAP("h2T", h2T_early[:], [128, 8, 512], BF16, [Bh2e])
                w2_load(6, [Bmg[c][2] for c in range(8)] + [Bh2e])
            nxt = b + 1 < NB
            if nxt:
                prep_front(b + 1, 0)
                prep_front(b + 1, 1)
            if b == 0:
                for tl in range(3):
                    ff2_tile(0, tl, yb=2 * tl, phase="A")
                ff2_tile(0, 0, yb=0, phase="B")
                prep_back(1, 0)
                prep_front(1, 2)
                ff2_tile(0, 1, yb=2, phase="B")
                prep_back(1, 1)
                prep_front(1, 3)
                ff2_tile(0, 2, yb=4, phase="B")
                prep_back(1, 2)
                prep_back(1, 3)
                ff2_tile(0, 3, yb=0)
                continue
            ff2_tile(b, 0)
            if nxt:
                prep_back(b + 1, 0)
                prep_front(b + 1, 2)
            ff2_tile(b, 1)
            if nxt:
                prep_back(b + 1, 1)
                prep_front(b + 1, 3)
            ff2_tile(b, 2)
            if nxt:
                prep_back(b + 1, 2)
                prep_back(b + 1, 3)
            ff2_tile(b, 3)

        with nc.Block() as block:
            S.emit(block, sems, final_wait_keys=[k_xs[0], k_xs[1]] + dbg_keys)
    return nc


_NC_CACHE = {}


def _col(v, n):
    return np.ascontiguousarray(np.asarray(v, np.float32).reshape(n, 128).T)


def make_in_maps(x, c, positions, ada_w, ada_b, mix_pre_g, mix_post_g, ffn_pre_g, ffn_post_g,
                 w_in, b_branch_gate, pool_w, pool_scale, w_branch_pool, w_branch_ret, w_out, w_ff1, w_ff2):
    cf, cb, cb2 = _consts()
    f = lambda a: np.ascontiguousarray(np.asarray(a, np.float32))
    shared = {
        "cf": cf, "cb": cb, "cb2": cb2,
        "ada_w": f(ada_w[0]), "w_in": f(w_in[0]),
        "pool_w": np.ascontiguousarray(np.asarray(pool_w[0], np.float32).transpose(1, 0, 2)),
        "w_bp": f(w_branch_pool[0]), "w_br": f(w_branch_ret[0]), "w_out": f(w_out[0]),
        "w_ff1": f(w_ff1[0]), "w_ff2": f(w_ff2[0]),
    }
    maps = []
    for b in range(8):
        cc = np.zeros((128, CC_N), np.float32)
        cc[:, CC_C:CC_C + 8] = _col(c[b], 8)
        cc[:, CC_ADAB:CC_ADAB + 48] = _col(ada_b[0], 48)
        cc[:, CC_GPM:CC_GPM + 8] = _col(mix_pre_g[0], 8)
        cc[:, CC_GQM:CC_GQM + 8] = _col(mix_post_g[0], 8)
        cc[:, CC_GPF:CC_GPF + 8] = _col(ffn_pre_g[0], 8)
        cc[:, CC_GQF:CC_GQF + 8] = _col(ffn_post_g[0], 8)
        cc[:, CC_BG:CC_BG + 16] = _col(b_branch_gate[0], 16)
        cc[:, CC_PS:CC_PS + 4] = _col(pool_scale[0], 4)
        m = dict(shared)
        m["x"] = f(x[b])
        m["pos"] = np.ascontiguousarray(np.asarray(positions[b], np.int32).reshape(NT, 128).T)
        m["cc"] = cc
        maps.append(m)
    return maps


def kernel(**inputs):
    if "nc" not in _NC_CACHE:
        _NC_CACHE["nc"] = build()
    nc = _NC_CACHE["nc"]
    maps = make_in_maps(**inputs)
    res = run_bass_kernel_spmd(nc, maps, core_ids=list(range(8)))
    return np.stack([np.asarray(r["out"], np.float32) for r in res.results], axis=0)
```
